# Optimizing a Trainium2 kernel written in Bass

```python
import jax, jax.numpy as jnp
from jax import lax
import numpy as np

D_MODEL = 1024
BATCH = 32
SEQ = 2048
DEPTH = 4

N_MIXERS = 2
EPS = 1e-6
POOL_WINDOWS = (2, 4, 8, 16)
N_POOL_GROUPS = len(POOL_WINDOWS)
POOL_GROUP = D_MODEL // N_POOL_GROUPS
GLA_HEADS = 4
GLA_KEY_DIM = D_MODEL // 2
GLA_VAL_DIM = D_MODEL
GLA_DK = GLA_KEY_DIM // GLA_HEADS
GLA_DV = GLA_VAL_DIM // GLA_HEADS
GATE_RANK = 16
GATE_NORMALIZER = 16.0
CHUNK = 64
GLA_IN = 2 * GLA_KEY_DIM + 2 * GLA_VAL_DIM + 2 * GATE_RANK
D_FF = -(-8 * D_MODEL // (3 * 256)) * 256
N_POOL_LAYERS = (DEPTH + 1) // 2
N_GLA_LAYERS = DEPTH // 2

kernel_name = "hybrid_pool_gla_encoder"


def rms_norm(x, gain):
    xf = x.astype(jnp.float32)
    y = xf * lax.rsqrt(jnp.mean(xf * xf, axis=-1, keepdims=True) + EPS)
    return (y * gain.astype(jnp.float32)).astype(x.dtype)


def pool_mixer(h, w_group, scale):
    B, S, D = h.shape
    hf = h.astype(jnp.float32)
    csum = jnp.concatenate([jnp.zeros((B, 1, D), jnp.float32), jnp.cumsum(hf, axis=1)], axis=1)
    t = jnp.arange(S)
    outs = []
    for g, win in enumerate(POOL_WINDOWS):
        left = win // 2
        right = win - 1 - left
        lo = jnp.clip(t - left, 0, S)
        hi = jnp.clip(t + right + 1, 0, S)
        cs = csum[:, :, g * POOL_GROUP:(g + 1) * POOL_GROUP]
        cnt = (hi - lo).astype(jnp.float32)[None, :, None]
        mean = (jnp.take(cs, hi, axis=1) - jnp.take(cs, lo, axis=1)) / cnt
        outs.append(mean - hf[:, :, g * POOL_GROUP:(g + 1) * POOL_GROUP])
    mixed = jnp.stack(outs, axis=2).astype(h.dtype)
    y = jnp.einsum('bsgc,gcd->bsgd', mixed, w_group).reshape(B, S, D)
    return y * scale.astype(y.dtype)


def gla_chunked(q, k, v, log_a, strict):
    B, S, H, _ = q.shape
    n = S // CHUNK

    def blocks(a):
        return a.astype(jnp.float32).reshape(B, n, CHUNK, H, -1).transpose(1, 0, 3, 2, 4)

    qb, kb, vb, gb = blocks(q), blocks(k), blocks(v), blocks(log_a)
    b = jnp.cumsum(gb, axis=-2)
    b_last = b[..., -1:, :]
    q_dec = qb * jnp.exp(b)
    k_inv = kb * jnp.exp(-b)
    k_dec = kb * jnp.exp(b_last - b)
    mask = jnp.tril(jnp.ones((CHUNK, CHUNK), bool), k=-1 if strict else 0)
    scores = jnp.where(mask, jnp.einsum('nbhcd,nbhsd->nbhcs', q_dec, k_inv), 0.0)
    o_intra = jnp.einsum('nbhcs,nbhsv->nbhcv', scores, vb)

    def step(state, xs):
        qd, kd, vv, bl = xs
        o = jnp.einsum('bhcd,bhdv->bhcv', qd, state)
        state = jnp.exp(bl)[..., 0, :, None] * state + jnp.einsum('bhcd,bhcv->bhdv', kd, vv)
        return state, o

    s0 = jnp.zeros((B, H, qb.shape[-1], vb.shape[-1]), jnp.float32)
    _, o_inter = lax.scan(step, s0, (q_dec, k_dec, vb, b_last))
    o = o_intra + o_inter
    return o.transpose(1, 0, 3, 2, 4).reshape(B, S, H, vb.shape[-1])


def gla_mixer(h, w_in, w_gate_up, b_gate, head_gain, w_out):
    B, S, D = h.shape
    proj = h @ w_in
    i1 = GLA_KEY_DIM
    i2 = i1 + GLA_KEY_DIM
    i3 = i2 + GLA_VAL_DIM
    i4 = i3 + GLA_VAL_DIM
    i5 = i4 + GATE_RANK
    q = proj[..., :i1].reshape(B, S, GLA_HEADS, GLA_DK) * (GLA_DK ** -0.5)
    k = proj[..., i1:i2].reshape(B, S, GLA_HEADS, GLA_DK)
    v = proj[..., i2:i3].reshape(B, S, GLA_HEADS, GLA_DV)
    r = proj[..., i3:i4].reshape(B, S, GLA_HEADS, GLA_DV)
    g_f = proj[..., i4:i5]
    g_b = proj[..., i5:]

    def log_gate(g_lr, w_up, bias):
        z = (g_lr @ w_up + bias).astype(jnp.float32)
        return (jax.nn.log_sigmoid(z) / GATE_NORMALIZER).reshape(B, S, GLA_HEADS, GLA_DK)

    la_f = log_gate(g_f, w_gate_up[0], b_gate[0])
    la_b = log_gate(g_b, w_gate_up[1], b_gate[1])
    o_fwd = gla_chunked(q, k, v, la_f, strict=False)
    flip = lambda a: jnp.flip(a, axis=1)
    o_bwd = flip(gla_chunked(flip(q), flip(k), flip(v), flip(la_b), strict=True))
    o = (o_fwd + o_bwd).astype(h.dtype)
    o = rms_norm(o, head_gain) * jax.nn.silu(r)
    return o.reshape(B, S, GLA_VAL_DIM) @ w_out


def swiglu(h, w_gate, w_up, w_down):
    return (jax.nn.silu(h @ w_gate) * (h @ w_up)) @ w_down


def setup_inputs(seed: int = 0) -> dict:
    key = jax.random.key(seed)
    ks = jax.random.split(key, 16)
    nrm = jax.random.normal
    f32 = jnp.float32
    return {
        "x": nrm(ks[0], (BATCH, SEQ, D_MODEL), f32),
        "norm_mix": 1.0 + 0.05 * nrm(ks[1], (DEPTH, D_MODEL), f32),
        "norm_ffn": 1.0 + 0.05 * nrm(ks[2], (DEPTH, D_MODEL), f32),
        "norm_final": 1.0 + 0.05 * nrm(ks[3], (D_MODEL,), f32),
        "w_pool": nrm(ks[4], (N_POOL_LAYERS, N_POOL_GROUPS, POOL_GROUP, POOL_GROUP), f32) * POOL_GROUP ** -0.5,
        "pool_scale": 1.0 + 0.05 * nrm(ks[5], (N_POOL_LAYERS, D_MODEL), f32),
        "w_gla_in": nrm(ks[6], (N_GLA_LAYERS, D_MODEL, GLA_IN), f32) * D_MODEL ** -0.5,
        "w_gate_up": nrm(ks[7], (N_GLA_LAYERS, 2, GATE_RANK, GLA_KEY_DIM), f32) * GATE_RANK ** -0.5,
        "b_gate": 0.1 * nrm(ks[8], (N_GLA_LAYERS, 2, GLA_KEY_DIM), f32),
        "gla_head_norm": 1.0 + 0.05 * nrm(ks[9], (N_GLA_LAYERS, GLA_DV), f32),
        "w_gla_out": nrm(ks[10], (N_GLA_LAYERS, GLA_VAL_DIM, D_MODEL), f32) * GLA_VAL_DIM ** -0.5,
        "w_ffn_gate": nrm(ks[11], (DEPTH, D_MODEL, D_FF), f32) * D_MODEL ** -0.5,
        "w_ffn_up": nrm(ks[12], (DEPTH, D_MODEL, D_FF), f32) * D_MODEL ** -0.5,
        "w_ffn_down": nrm(ks[13], (DEPTH, D_FF, D_MODEL), f32) * D_FF ** -0.5,
    }


def reference(x, norm_mix, norm_ffn, norm_final, w_pool, pool_scale, w_gla_in, w_gate_up, b_gate,
              gla_head_norm, w_gla_out, w_ffn_gate, w_ffn_up, w_ffn_down):
    for i in range(DEPTH):
        j = i // N_MIXERS
        h = rms_norm(x, norm_mix[i])
        if i % N_MIXERS == 0:
            x = x + pool_mixer(h, w_pool[j], pool_scale[j])
        else:
            x = x + gla_mixer(h, w_gla_in[j], w_gate_up[j], b_gate[j], gla_head_norm[j], w_gla_out[j])
        h = rms_norm(x, norm_ffn[i])
        x = x + swiglu(h, w_ffn_gate[i], w_ffn_up[i], w_ffn_down[i])
    return rms_norm(x, norm_final)
```

```python
import numpy as np
from contextlib import ExitStack
import concourse.bass as bass
import concourse.mybir as mybir
from concourse.bass_utils import run_bass_kernel_spmd

F32 = mybir.dt.float32
BF16 = mybir.dt.bfloat16
AF = mybir.ActivationFunctionType
ALU = mybir.AluOpType

S = 2048
D = 1024
KC = 8
DFF = 2816
FC = 22
NTT = 4
TW = 512
DEPTH = 4
EPS = 1e-6
GLA_IN = 3104
N_CORES = 8
FFN_GROUPS = [(0, 4), (4, 4), (8, 4), (12, 4), (16, 3), (19, 3)]


class FW:
    ENG = ("pe", "act", "dve", "pool", "sp")

    def __init__(self, nc, stack):
        self.nc, self.stack = nc, stack
        self.prog = {e: [] for e in self.ENG}
        self.sem, self.cnt = {}, {}
        self.seen = {e: {} for e in self.ENG}
        self.lastw, self.readers = {}, {}
        self.nsem = 0
        self.pe_sems = set()
        self.dma_pool, self.dma_rr = {}, {}
        self.new_epoch()

    def _newsem(self, name):
        self.nsem += 1
        return self.stack.enter_context(self.nc.semaphore(f"{name}{self.nsem}"))

    def new_epoch(self):
        for e in ("pe", "act", "dve", "pool"):
            self.sem[e] = self._newsem(e)
            self.cnt[e] = 0
        self.pe_sems.add(id(self.sem["pe"]))

    @staticmethod
    def _add(d, t):
        if t is None:
            return
        s, v = t
        k = id(s)
        if k not in d or d[k][1] < v:
            d[k] = (s, v)

    def _deps(self, reads, writes, e=None):
        d = {}
        for k in reads:
            for t in self.lastw.get(k, {}).values():
                self._add(d, t)
            if isinstance(k, tuple) and k[0] == "ps":
                own = id(self.sem.get(e)) if e in self.sem else None
                for t in self.readers.get(k, {}).values():
                    if id(t[0]) != own:
                        self._add(d, t)
        for k in writes:
            for t in self.lastw.get(k, {}).values():
                self._add(d, t)
            for t in self.readers.get(k, {}).values():
                self._add(d, t)
        return d

    def _waits(self, e, d):
        w = []
        for k, (s, v) in d.items():
            if e == "pe" and k in self.pe_sems:
                continue
            if self.seen[e].get(k, 0) >= v:
                continue
            self.seen[e][k] = v
            w.append((s, v))
        return w

    def _commit(self, tok, reads, writes):
        for k in writes:
            self._add(self.lastw.setdefault(k, {}), tok)
        for k in reads:
            self._add(self.readers.setdefault(k, {}), tok)

    def op(self, e, fn, reads=(), writes=()):
        w = self._waits(e, self._deps(reads, writes, e))
        self.cnt[e] += 1
        assert self.cnt[e] < 30000
        tok = (self.sem[e], self.cnt[e])
        self.prog[e].append((w, fn, tok, 1))
        self._commit(tok, reads, writes)
        return tok

    def pe_group(self, fns, reads, writes):
        w = self._waits("pe", self._deps(reads, writes))
        tok = None
        for i, fn in enumerate(fns):
            if i == len(fns) - 1:
                self.cnt["pe"] += 1
                assert self.cnt["pe"] < 30000
                tok = (self.sem["pe"], self.cnt["pe"])
            self.prog["pe"].append((w if i == 0 else [], fn, tok if i == len(fns) - 1 else None, 1))
        self._commit(tok, reads, writes)
        return tok

    def dma(self, q, out, in_, reads=(), writes=()):
        NS = 8
        pool = self.dma_pool.setdefault(q, [])
        i = self.dma_rr.get(q, 0)
        self.dma_rr[q] = i + 1
        if len(pool) < NS:
            pool.append([self._newsem("dma" + q), 0])
        ent = pool[i % NS]
        d = self._deps(reads, writes)
        if ent[1] > 0:
            self._add(d, (ent[0], ent[1]))
        w = self._waits(q, d)
        ent[1] += 16
        assert ent[1] < 30000
        tok = (ent[0], ent[1])
        self.prog[q].append((w, (lambda eng, o=out, i_=in_: eng.dma_start(out=o, in_=i_)), tok, 16))
        self._commit(tok, reads, writes)
        return tok

    def barrier(self):
        toks = [(self.sem[e], self.cnt[e]) for e in ("pe", "act", "dve", "pool") if self.cnt[e] > 0]
        for q, pool in self.dma_pool.items():
            toks += [(s, v) for s, v in pool if v > 0]
        for e in self.ENG:
            d = {}
            for t in toks:
                self._add(d, t)
            w = self._waits(e, d)
            if w:
                self.prog[e].append((w, None, None, 0))
        self.lastw.clear()
        self.readers.clear()

    def replay(self, e, eng):
        for w, fn, tok, inc in self.prog[e]:
            for s, v in w:
                eng.wait_ge(s, v)
            if fn is None:
                continue
            ins = fn(eng)
            if tok is not None:
                ins.then_inc(tok[0], inc)


class Alloc:
    def __init__(self, nc, base, limit):
        self.nc, self.off, self.limit = nc, base, limit

    def t(self, name, shape, dtype):
        n = 1
        for s_ in shape[1:]:
            n *= s_
        nbytes = n * (4 if dtype == F32 else 2)
        off = (self.off + 31) // 32 * 32
        assert off + nbytes <= self.limit, (name, off, nbytes, self.limit)
        self.off = off + nbytes
        return self.nc.alloc_sbuf_tensor_at(name, shape, dtype, offset=off)


def build(n_seq=4, layers=(0, 1, 2, 3)):
    nc = bass.Bass("TRN2", target_bir_lowering=False)
    stack = ExitStack()
    fw = FW(nc, stack)

    def din(name, shape):
        return nc.dram_tensor(name, shape, F32, kind="ExternalInput").ap()

    x_d = din("x", [n_seq, S, D])
    nmix_d = din("norm_mix", [DEPTH, D])
    nffn_d = din("norm_ffn", [DEPTH, D])
    nfin_d = din("norm_final", [D])
    wpool_d = din("w_pool", [2, 4, 256, 256])
    pscale_d = din("pool_scale", [2, D])
    win_d = din("w_gla_in", [2, D, GLA_IN])
    wgu_d = din("w_gate_up", [2, 2, 16, 512])
    bg_d = din("b_gate", [2, 2, 512])
    hg_d = din("gla_head_norm", [2, 256])
    wout_d = din("w_gla_out", [2, D, D])
    wfg_d = din("w_ffn_gate", [DEPTH, D, DFF])
    wfu_d = din("w_ffn_up", [DEPTH, D, DFF])
    wfd_d = din("w_ffn_down", [DEPTH, DFF, D])
    y_d = nc.dram_tensor("y", [n_seq, S, D], F32, kind="ExternalOutput").ap()

    LIMIT = 229376
    A = Alloc(nc, 16512, LIMIT)
    xT = A.t("xT", [128, KC, S], F32)
    hT = A.t("hT", [128, KC, S], BF16)
    ident = A.t("ident", [128, 128], F32)
    ones_d = A.t("ones_d", [128, 128], BF16)
    ones_v = A.t("ones_v", [128, 128], BF16)
    triFf = A.t("triFf", [128, 128], F32)
    triFb = A.t("triFb", [128, 128], F32)
    triTf = A.t("triTf", [128, 128], F32)
    triTb = A.t("triTb", [128, 128], F32)
    mskF = A.t("mskF", [128, 128], F32)
    mskB = A.t("mskB", [128, 128], F32)
    gains = A.t("gains", [128, 128], F32)
    halfm = A.t("halfm", [128, 2], F32)
    pinv = A.t("pinv", [128, 4, 16], F32)
    ARENA = (A.off + 31) // 32 * 32

    GMIX, GFFN, GFIN, PSC, HG = 0, 32, 64, 72, 88

    a = Alloc(nc, ARENA, LIMIT)
    xin = [a.t(f"xin{i}", [128, D], F32) for i in range(2)]
    ystg = [a.t(f"ystg{i}", [128, D], F32) for i in range(2)]
    ytmp = a.t("ytmp", [128, KC, TW], F32)
    n_sq = a.t("n_sq", [128, KC, TW], BF16)
    n_srt = a.t("n_srt", [128, TW], F32)
    n_rstd = a.t("n_rstd", [128, TW], F32)
    stg = a.t("stg", [128, 128], F32)
    a = Alloc(nc, ARENA, LIMIT)
    f_sq = a.t("f_sq", [128, KC, TW], BF16)
    f_srt = a.t("f_srt", [128, TW], F32)
    f_rstd = a.t("f_rstd", [128, TW], F32)
    f_wg = [a.t(f"f_wg{i}", [128, KC, 512], BF16) for i in range(2)]
    f_wu = [a.t(f"f_wu{i}", [128, KC, 512], BF16) for i in range(2)]
    f_wd = [a.t(f"f_wd{i}", [128, 4, D], BF16) for i in range(2)]
    f_act = a.t("f_act", [128, 4, S], BF16)
    f_sg = [a.t(f"f_sg{i}", [128, TW], F32) for i in range(2)]
    a = Alloc(nc, ARENA, LIMIT)
    p_sq = a.t("p_sq", [128, KC, TW], BF16)
    p_srt = a.t("p_srt", [128, TW], F32)
    p_rstd = a.t("p_rstd", [128, S], F32)
    HW_ = S + 16
    p_hA = a.t("p_hA", [128, HW_], F32)
    p_hB1 = a.t("p_hB1", [128, HW_], F32)
    p_hB2 = a.t("p_hB2", [128, HW_], F32)
    p_mix = a.t("p_mix", [128, 2, S], BF16)
    p_w = a.t("p_w", [128, 4, 2, 256], BF16)
    p_fix = a.t("p_fix", [128, 16], F32)
    a = Alloc(nc, ARENA, LIMIT)
    g_sq = a.t("g_sq", [128, KC, TW], BF16)
    g_srt = a.t("g_srt", [128, TW], F32)
    g_rstd = a.t("g_rstd", [128, TW], F32)
    a = Alloc(nc, ARENA, LIMIT)
    g_wq = a.t("g_wq", [128, KC, 128], BF16)
    g_wkv = a.t("g_wkv", [128, KC, 384], BF16)
    g_wr = a.t("g_wr", [128, KC, 256], BF16)
    g_wo = a.t("g_wo", [128, 2, D], BF16)
    g_wgate = a.t("g_wgate", [128, KC, 32], BF16)
    g_wz = [a.t(f"g_wz{i}", [33, 512], BF16) for i in range(2)]
    g_gT = a.t("g_gT", [33, S], BF16)
    g_qd = [a.t(f"g_qd{i}", [128, S], BF16) for i in range(2)]
    g_ki = [a.t(f"g_ki{i}", [128, S], BF16) for i in range(2)]
    g_kd = [[a.t(f"g_kd{i}{h}", [128, 16, 128], BF16) for h in range(2)] for i in range(2)]
    g_v = a.t("g_v", [128, 16, 256], BF16)
    g_oacc = a.t("g_oacc", [128, 2, S], F32)
    g_dec = [a.t(f"g_dec{i}", [128, 32], F32) for i in range(2)]
    g_go = g_qd[0]
    G_T = (a.off + 31) // 32 * 32
    a2 = Alloc(nc, G_T, LIMIT)
    g_e = [a2.t(f"g_e{i}", [128, TW], F32) for i in range(2)]
    g_sp = [a2.t(f"g_sp{i}", [128, 4, 128], F32) for i in range(2)]
    g_Eq = [a2.t(f"g_Eq{i}", [128, TW], F32) for i in range(2)]
    g_Ek = [a2.t(f"g_Ek{i}", [128, TW], F32) for i in range(2)]
    g_Ekd = [a2.t(f"g_Ekd{i}", [128, 4, 128], F32) for i in range(2)]
    a2 = Alloc(nc, G_T, LIMIT)
    g_sm = [[a2.t(f"g_sm{i}{r}", [128, 128], BF16) for r in range(2)] for i in range(2)]
    g_S32 = [[a2.t(f"g_S32{i}{r}", [128, 256], F32) for r in range(2)] for i in range(2)]
    g_Sbf = [[a2.t(f"g_Sbf{i}{r}", [128, 256], BF16) for r in range(3)] for i in range(2)]
    a2 = Alloc(nc, G_T, LIMIT)
    g_sq2 = a2.t("g_sq2", [128, 2, TW], BF16)
    g_srt2 = a2.t("g_srt2", [128, TW], F32)
    g_rstd2 = a2.t("g_rstd2", [128, TW], F32)
    g_sr = [a2.t(f"g_sr{i}", [128, TW], F32) for i in range(2)]
    g_t1 = a2.t("g_t1", [128, TW], F32)
    g_go = nc.alloc_sbuf_tensor_at("g_go", [128, 2, S], BF16, offset=g_qd[0].manual_sbuf_range[0])
    assert g_qd[1].manual_sbuf_range[0] == g_qd[0].manual_sbuf_range[0] + S * 2

    banks = [stack.enter_context(nc.psum_tensor(f"psb{i}", [128, 512], F32)) for i in range(8)]
    bank_rr = [0]

    def nbank():
        i = bank_rr[0] % 8
        bank_rr[0] += 1
        return i, banks[i]

    def v3(ap, b):
        return ap.rearrange("p (a b) -> p a b", b=b)

    def mm(out, lhsT, rhs, start, stop):
        return lambda pe: pe.matmul(out, lhsT=lhsT, rhs=rhs, start=start, stop=stop)

    def tr(out, in_):
        return lambda pe: pe.transpose(out, in_, ident[:])

    def act(out, in_, func, reads, writes, bias=None, scale=None):
        kw = {}
        if bias is not None:
            kw["bias"] = bias
        if scale is not None:
            kw["scale"] = scale
        return fw.op("act", lambda e: e.activation(out=out, in_=in_, func=func, **kw), reads, writes)

    def copy(eng, out, in_, reads, writes):
        if eng == "act":
            return fw.op("act", lambda e: e.activation(out=out, in_=in_, func=AF.Copy), reads, writes)
        return fw.op(eng, lambda e: e.tensor_copy(out=out, in_=in_), reads, writes)

    def stt(out, in0, scalar, in1, op0, op1, reads, writes, eng="dve"):
        return fw.op(eng, lambda e: e.scalar_tensor_tensor(out=out, in0=in0, scalar=scalar, in1=in1, op0=op0, op1=op1),
                     reads, writes)

    def tt_(out, in0, in1, op, reads, writes, eng="dve"):
        return fw.op(eng, lambda e: e.tensor_tensor(out=out, in0=in0, in1=in1, op=op), reads, writes)

    def memset(eng, ap, val, writes):
        return fw.op(eng, lambda e: e.memset(ap, val), (), writes)

    def xk(ks, tts):
        return [("xT", k, t) for k in ks for t in tts]

    def hk(ks, tts):
        return [("hT", k, t) for k in ks for t in tts]

    ALLK = list(range(KC))
    ALLT = list(range(NTT))

    def setup_consts():
        memset("pool", ident[:], 0.0, ["ident"])
        fw.op("pool", lambda e: e.affine_select(out=ident[:], in_=ident[:], pattern=[[-1, 128]],
                                                compare_op=ALU.not_equal, fill=1.0, base=0, channel_multiplier=1),
              ["ident"], ["ident"])
        memset("pool", ones_d[:], 1.0 / D, ["ones_d"])
        memset("pool", ones_v[:], 1.0 / 256, ["ones_v"])
        specs = [
            (triFf, -1.0 / 16, -1, 1, ALU.is_ge),
            (triFb, -1.0 / 16, 1, -1, ALU.is_ge),
            (triTf, -1.0 / 16, 1, -1, ALU.is_gt),
            (triTb, -1.0 / 16, -1, 1, ALU.is_gt),
            (mskF, 1.0, -1, 1, ALU.is_ge),
            (mskB, 1.0, 1, -1, ALU.is_gt),
        ]
        for i, (t, val, cm, pc, cmp_) in enumerate(specs):
            key = ("tri", i)
            memset("pool", t[:], val, [key])
            fw.op("pool", lambda e, t=t, cm=cm, pc=pc, cmp_=cmp_: e.affine_select(
                out=t[:], in_=t[:], pattern=[[pc, 128]], compare_op=cmp_, fill=0.0, base=0, channel_multiplier=cm),
                [key], [key])
            memset("pool", t[0:64, 64:128], 0.0, [key])
            memset("pool", t[64:128, 0:64], 0.0, [key])
        memset("pool", halfm[:], 0.0, ["halfm"])
        memset("pool", halfm[0:64, 0:1], 1.0, ["halfm"])
        memset("pool", halfm[64:128, 1:2], 1.0, ["halfm"])
        memset("pool", pinv[:], 1.0, ["pinv"])
        for g, win in enumerate((2, 4, 8, 16)):
            left = win // 2
            right = win - 1 - left
            for t in range(left):
                memset("pool", pinv[:, g, t:t + 1], 1.0 / (t + right + 1), ["pinv"])
            for r in range(right):
                t = S - right + r
                memset("pool", pinv[:, g, 8 + r:9 + r], 1.0 / (S - t + left), ["pinv"])
        memset("pool", stg[:], 0.0, ["stg"])
        fw.dma("sp", stg[GMIX:GMIX + 32, :], nmix_d.rearrange("i (k p) -> (i k) p", p=128), (), ["stg"])
        fw.dma("sp", stg[GFFN:GFFN + 32, :], nffn_d.rearrange("i (k p) -> (i k) p", p=128), (), ["stg"])
        fw.dma("sp", stg[GFIN:GFIN + 8, :], nfin_d.rearrange("(k p) -> k p", p=128), (), ["stg"])
        fw.dma("sp", stg[PSC:PSC + 16, :], pscale_d.rearrange("i (k p) -> (i k) p", p=128), (), ["stg"])
        fw.dma("sp", stg[HG:HG + 4, :], hg_d.rearrange("i (k p) -> (i k) p", p=128), (), ["stg"])
        bi, b = nbank()
        fw.pe_group([tr(b[:, 0:128], stg[:])], ["stg", "ident"], [("ps", bi)])
        copy("dve", gains[:], b[:, 0:128], [("ps", bi)], ["gains"])
        fw.barrier()

    def load_x(s):
        for j in range(16):
            sl = j % 2
            fw.dma("sp", xin[sl][:], x_d[s, j * 128:(j + 1) * 128, :], (), [("xin", sl)])
            for half in range(2):
                bi, b = nbank()
                bv = v3(b[:], 128)
                fw.pe_group([tr(bv[:, kk, :], xin[sl][:, (half * 4 + kk) * 128:(half * 4 + kk + 1) * 128])
                             for kk in range(4)], [("xin", sl), "ident"], [("ps", bi)])
                copy("act" if half == 0 else "dve", xT[:, half * 4:half * 4 + 4, j * 128:(j + 1) * 128], bv,
                     [("ps", bi)], xk(range(half * 4, half * 4 + 4), [j // 4]))

    def rstd_tile(tt, sq, srt, rstd_out, rkey):
        sl = slice(tt * TW, (tt + 1) * TW)
        act(sq[:], xT[:, :, sl], AF.Square, xk(ALLK, [tt]), ["n_sq"])
        bi, b = nbank()
        fw.pe_group([mm(b[:], ones_d[:], sq[:, k, :], k == 0, k == KC - 1) for k in range(KC)],
                    ["n_sq", "ones_d"], [("ps", bi)])
        act(srt[:], b[:], AF.Sqrt, [("ps", bi)], ["n_srt"], bias=EPS)
        fw.op("dve", lambda e: e.reciprocal(out=rstd_out, in_=srt[:]), ["n_srt"], [rkey])

    def main_norm(gcol, sq, srt, rstd):
        for tt in range(NTT):
            sl = slice(tt * TW, (tt + 1) * TW)
            rstd_tile(tt, sq, srt, rstd[:], "n_rstd")
            for k in range(KC):
                stt(hT[:, k, sl], xT[:, k, sl], gains[:, gcol + k:gcol + k + 1], rstd[:], ALU.mult, ALU.mult,
                    xk([k], [tt]) + ["n_rstd", "gains"], hk([k], [tt]))

    def final_store(s):
        for tt in range(NTT):
            sl = slice(tt * TW, (tt + 1) * TW)
            rstd_tile(tt, n_sq, n_srt, n_rstd[:], "n_rstd")
            for k in range(KC):
                stt(ytmp[:, k, :], xT[:, k, sl], gains[:, GFIN + k:GFIN + k + 1], n_rstd[:], ALU.mult, ALU.mult,
                    xk([k], [tt]) + ["n_rstd", "gains"], [("ytmp", k)])
            for jj in range(4):
                j = tt * 4 + jj
                sl_ = j % 2
                for half in range(2):
                    bi, b = nbank()
                    bv = v3(b[:], 128)
                    fw.pe_group([tr(bv[:, kk, :], ytmp[:, half * 4 + kk, jj * 128:(jj + 1) * 128]) for kk in range(4)],
                                [("ytmp", half * 4 + kk) for kk in range(4)] + ["ident"], [("ps", bi)])
                    copy("act" if half == 0 else "dve", ystg[sl_][:, half * 512:(half + 1) * 512], b[:],
                         [("ps", bi)], [("ystg", sl_, half)])
                fw.dma("sp", y_d[s, j * 128:(j + 1) * 128, :], ystg[sl_][:], [("ystg", sl_, 0), ("ystg", sl_, 1)], ())

    def ffn(i):
        main_norm(GFFN + i * 8 - 0, f_sq, f_srt, f_rstd)
        for gi, (c0, ncn) in enumerate(FFN_GROUPS):
            sl_ = gi % 2
            f0, f1 = c0 * 128, (c0 + ncn) * 128
            fw.dma("pool", f_wg[sl_][:, :, 0:ncn * 128], wfg_d[i].rearrange("(k p) f -> p k f", p=128)[:, :, f0:f1],
                   (), [("wg", sl_)])
            fw.dma("pool", f_wu[sl_][:, :, 0:ncn * 128], wfu_d[i].rearrange("(k p) f -> p k f", p=128)[:, :, f0:f1],
                   (), [("wu", sl_)])
            fw.dma("pool", f_wd[sl_][:, 0:ncn, :], wfd_d[i][f0:f1, :].rearrange("(c p) d -> p c d", p=128),
                   (), [("wd", sl_)])
            for c in range(ncn):
                for tt in range(NTT):
                    sl = slice(tt * TW, (tt + 1) * TW)
                    big, bg = nbank()
                    fw.pe_group([mm(bg[:], f_wg[sl_][:, k, c * 128:(c + 1) * 128], hT[:, k, sl], k == 0, k == KC - 1)
                                 for k in range(KC)], hk(ALLK, [tt]) + [("wg", sl_)], [("ps", big)])
                    biu, bu = nbank()
                    fw.pe_group([mm(bu[:], f_wu[sl_][:, k, c * 128:(c + 1) * 128], hT[:, k, sl], k == 0, k == KC - 1)
                                 for k in range(KC)], hk(ALLK, [tt]) + [("wu", sl_)], [("ps", biu)])
                    sgi = (c * NTT + tt) % 2
                    act(f_sg[sgi][:], bg[:], AF.Silu, [("ps", big)], [("sg", sgi)])
                    tt_(f_act[:, c, sl], f_sg[sgi][:], bu[:], ALU.mult, [("sg", sgi), ("ps", biu)], [("act", c, tt)])
            for tt in range(NTT):
                sl = slice(tt * TW, (tt + 1) * TW)
                for dc in range(KC):
                    bi, b = nbank()
                    fw.pe_group([mm(b[:], f_wd[sl_][:, c, dc * 128:(dc + 1) * 128], f_act[:, c, sl], c == 0, c == ncn - 1)
                                 for c in range(ncn)], [("act", c, tt) for c in range(ncn)] + [("wd", sl_)], [("ps", bi)])
                    tt_(xT[:, dc, sl], xT[:, dc, sl], b[:], ALU.add, xk([dc], [tt]) + [("ps", bi)], xk([dc], [tt]))

    def pool_mixer(i):
        j = i // 2
        fw.dma("pool", p_w[:], wpool_d[j].rearrange("g (c p) d -> p g c d", p=128), (), ["p_w"])
        memset("pool", p_hA[:, 0:8], 0.0, ["hA"])
        memset("pool", p_hA[:, 8 + S:16 + S], 0.0, ["hA"])
        for tt in range(NTT):
            rstd_tile(tt, p_sq, p_srt, p_rstd[:, tt * TW:(tt + 1) * TW], "p_rstd")
        for g, win in enumerate((2, 4, 8, 16)):
            left = win // 2
            right = win - 1 - left
            for cc in range(2):
                k = 2 * g + cc
                stt(p_hA[:, 8:8 + S], xT[:, k, :], gains[:, GMIX + i * 8 + k:GMIX + i * 8 + k + 1], p_rstd[:],
                    ALU.mult, ALU.mult, xk([k], ALLT) + ["p_rstd", "gains"], ["hA"])
                tt_(p_hB1[:, 1:HW_], p_hA[:, 1:HW_], p_hA[:, 0:HW_ - 1], ALU.add, ["hA"], ["hB1"])
                src = p_hB1
                skey = "hB1"
                if win >= 4:
                    tt_(p_hB2[:, 2:HW_ - 1], p_hB1[:, 3:HW_], p_hB1[:, 1:HW_ - 2], ALU.add, ["hB1"], ["hB2"])
                    src, skey = p_hB2, "hB2"
                if win >= 8:
                    tt_(p_hB1[:, 4:HW_ - 3], p_hB2[:, 6:HW_ - 1], p_hB2[:, 2:HW_ - 5], ALU.add, ["hB2"], ["hB1"])
                    src, skey = p_hB1, "hB1"
                if win >= 16:
                    tt_(p_hB2[:, 8:8 + S], p_hB1[:, 12:12 + S], p_hB1[:, 4:4 + S], ALU.add, ["hB1"], ["hB2"])
                    src, skey = p_hB2, "hB2"
                stt(p_mix[:, cc, :], src[:, 8:8 + S], 1.0 / win, p_hA[:, 8:8 + S], ALU.mult, ALU.subtract,
                    [skey, "hA"], [("mix", cc)])
                tt_(p_fix[:, 0:left], src[:, 8:8 + left], pinv[:, g, 0:left], ALU.mult, [skey, "pinv"], ["p_fix"])
                tt_(p_mix[:, cc, 0:left], p_fix[:, 0:left], p_hA[:, 8:8 + left], ALU.subtract,
                    ["p_fix", "hA"], [("mix", cc)])
                if right > 0:
                    tt_(p_fix[:, 8:8 + right], src[:, 8 + S - right:8 + S], pinv[:, g, 8:8 + right], ALU.mult,
                        [skey, "pinv"], ["p_fix"])
                    tt_(p_mix[:, cc, S - right:S], p_fix[:, 8:8 + right], p_hA[:, 8 + S - right:8 + S], ALU.subtract,
                        ["p_fix", "hA"], [("mix", cc)])
            for tt in range(NTT):
                sl = slice(tt * TW, (tt + 1) * TW)
                for dd in range(2):
                    k = 2 * g + dd
                    bi, b = nbank()
                    fw.pe_group([mm(b[:], p_w[:, g, cc, dd * 128:(dd + 1) * 128], p_mix[:, cc, sl], cc == 0, cc == 1)
                                 for cc in range(2)], [("mix", 0), ("mix", 1), "p_w"], [("ps", bi)])
                    stt(xT[:, k, sl], b[:], gains[:, PSC + j * 8 + k:PSC + j * 8 + k + 1], xT[:, k, sl],
                        ALU.mult, ALU.add, xk([k], [tt]) + [("ps", bi), "gains"], xk([k], [tt]))

    def gla_mixer(i):
        j = i // 2
        main_norm(GMIX + i * 8, g_sq, g_srt, g_rstd)
        fw.barrier()
        winv = win_d[j].rearrange("(k p) f -> p k f", p=128)
        fw.dma("pool", g_wgate[:], winv[:, :, 3072:3104], (), ["wgate"])
        for d_ in range(2):
            memset("pool", g_wz[d_][0:32, :], 0.0, [("wz", d_)])
            fw.dma("pool", g_wz[d_][16 * d_:16 * d_ + 16, :], wgu_d[j, d_], (), [("wz", d_)])
            fw.dma("pool", g_wz[d_][32:33, :], bg_d[j, d_:d_ + 1, :], (), [("wz", d_)])
        memset("pool", g_gT[32:33, :], 1.0, ["gT"])
        memset("pool", g_kd[0][0][:], 0.0, ["kd"])
        memset("pool", g_kd[0][1][:], 0.0, ["kd"])
        memset("pool", g_kd[1][0][:], 0.0, ["kd"])
        memset("pool", g_kd[1][1][:], 0.0, ["kd"])
        for tt in range(NTT):
            sl = slice(tt * TW, (tt + 1) * TW)
            bi, b = nbank()
            fw.pe_group([mm(b[0:32, :], g_wgate[:, k, :], hT[:, k, sl], k == 0, k == KC - 1) for k in range(KC)],
                        hk(ALLK, [tt]) + ["wgate"], [("ps", bi)])
            copy("act", g_gT[0:32, sl], b[0:32, :], [("ps", bi)], ["gT"])

        tris_f = (triFf, triFb)
        tris_t = (triTf, triTb)
        msks = (mskF, mskB)
        for h in range(4):
            fw.dma("pool", g_wq[:], winv[:, :, h * 128:(h + 1) * 128], (), ["wq"])
            fw.dma("pool", g_wkv[:, :, 0:128], winv[:, :, 512 + h * 128:512 + (h + 1) * 128], (), ["wkv"])
            fw.dma("pool", g_wkv[:, :, 128:384], winv[:, :, 1024 + h * 256:1024 + (h + 1) * 256], (), ["wkv"])
            fw.dma("pool", g_wr[:], winv[:, :, 2048 + h * 256:2048 + (h + 1) * 256], (), ["wr"])
            fw.dma("pool", g_wo[:], wout_d[j][h * 256:(h + 1) * 256, :].rearrange("(c p) d -> p c d", p=128), (), ["wo"])
            for tt in range(NTT):
                sl = slice(tt * TW, (tt + 1) * TW)
                for d_ in range(2):
                    bi, b = nbank()
                    bv = v3(b[:], 128)
                    for jj in range(4):
                        tsl = slice(tt * TW + jj * 128, tt * TW + (jj + 1) * 128)
                        fw.pe_group([mm(bv[:, jj, :], g_gT[0:33, tsl], g_wz[d_][0:33, h * 128:(h + 1) * 128], True, True)],
                                    ["gT", ("wz", d_)], [("ps", bi)])
                    act(g_e[d_][:], b[:], AF.Exp, [("ps", bi)], [("e", d_)], scale=-1.0)
                    act(g_sp[d_][:].rearrange("p a b -> p (a b)"), g_e[d_][:], AF.Ln, [("e", d_)], [("sp", d_)], bias=1.0)
                    bi, b = nbank()
                    bv = v3(b[:], 128)
                    for jj in range(4):
                        fw.pe_group([mm(bv[:, jj, :], g_sp[d_][:, jj, :], tris_f[d_][:], True, True)],
                                    [("sp", d_), ("tri", d_)], [("ps", bi)])
                    act(g_Eq[d_][:], b[:], AF.Exp, [("ps", bi)], [("Eq", d_)])
                    act(g_Ek[d_][:], b[:], AF.Exp, [("ps", bi)], [("Ek", d_)], scale=-1.0)
                    col = 63 if d_ == 0 else 0
                    copy("dve", g_dec[d_][:, tt * 8:(tt + 1) * 8], v3(g_Eq[d_][:], 64)[:, :, col], [("Eq", d_)], [("dec", d_)])
                    bi, b = nbank()
                    fw.pe_group([mm(b[:], tris_t[d_][:], g_sp[d_][:].rearrange("p a b -> p (a b)"), True, True)],
                                [("sp", d_), ("tri", 2 + d_)], [("ps", bi)])
                    act(g_Ekd[d_][:].rearrange("p a b -> p (a b)"), b[:], AF.Exp, [("ps", bi)], [("Ekd", d_)])
                bi, b = nbank()
                fw.pe_group([mm(b[:], g_wq[:, k, :], hT[:, k, sl], k == 0, k == KC - 1) for k in range(KC)],
                            hk(ALLK, [tt]) + ["wq"], [("ps", bi)])
                for d_ in range(2):
                    stt(g_qd[d_][:, sl], b[:], 128.0 ** -0.5, g_Eq[d_][:], ALU.mult, ALU.mult,
                        [("ps", bi), ("Eq", d_)], [("qd", d_)])
                bi, b = nbank()
                fw.pe_group([mm(b[:], g_wkv[:, k, 0:128], hT[:, k, sl], k == 0, k == KC - 1) for k in range(KC)],
                            hk(ALLK, [tt]) + ["wkv"], [("ps", bi)])
                for d_ in range(2):
                    tt_(g_ki[d_][:, sl], b[:], g_Ek[d_][:], ALU.mult, [("ps", bi), ("Ek", d_)], [("ki", d_)])
                for jj in range(4):
                    jt = tt * 4 + jj
                    tsl = slice(jt * 128, (jt + 1) * 128)
                    bi, b = nbank()
                    fw.pe_group([mm(b[:, 0:128], hT[:, k, tsl], g_wkv[:, k, 0:128], k == 0, k == KC - 1) for k in range(KC)],
                                hk(ALLK, [tt]) + ["wkv"], [("ps", bi)])
                    for d_ in range(2):
                        for hf in range(2):
                            stt(g_kd[d_][hf][:, jt, :], b[:, 0:128], halfm[:, hf:hf + 1], g_Ekd[d_][:, jj, :],
                                ALU.mult, ALU.mult, [("ps", bi), ("Ekd", d_), "halfm"], ["kd"])
                    bi, b = nbank()
                    fw.pe_group([mm(b[:, 0:256], hT[:, k, tsl], g_wkv[:, k, 128:384], k == 0, k == KC - 1) for k in range(KC)],
                                hk(ALLK, [tt]) + ["wkv"], [("ps", bi)])
                    copy("act", g_v[:, jt, :], b[:, 0:256], [("ps", bi)], ["v"])
            fw.barrier()
            for d_ in range(2):
                memset("dve", g_S32[d_][0][:], 0.0, [("S32", d_, 0)])
                memset("dve", g_Sbf[d_][0][:], 0.0, [("Sbf", d_, 0)])
            st_i = [0, 0]
            sb_i = [0, 0]
            for step in range(16):
                for d_ in range(2):
                    jt = step if d_ == 0 else 15 - step
                    tsl = slice(jt * 128, (jt + 1) * 128)
                    bi, b = nbank()
                    fw.pe_group([mm(b[:, 0:128], g_ki[d_][:, tsl], g_qd[d_][:, tsl], True, True)],
                                [("ki", d_), ("qd", d_)], [("ps", bi)])
                    smi = step % 2
                    sm = g_sm[d_][smi]
                    tt_(sm[:], b[:, 0:128], msks[d_][:], ALU.mult, [("ps", bi), ("tri", 4 + d_)], [("sm", d_, smi)])
                    biu, bu = nbank()
                    buv = v3(bu[:], 256)
                    for hf in range(2):
                        fw.pe_group([mm(buv[:, hf, :], g_kd[d_][hf][:, jt, :], g_v[:, jt, :], True, True)],
                                    ["kd", "v"], [("ps", biu)])
                    bio, bo = nbank()
                    bov = v3(bo[:, 0:256], 128)
                    order = (0, 1) if d_ == 0 else (1, 0)
                    for hf in order:
                        csl = slice(hf * 64, (hf + 1) * 64)
                        qsl = slice(jt * 128 + hf * 64, jt * 128 + (hf + 1) * 64)
                        sb = g_Sbf[d_][sb_i[d_]]
                        fns = []
                        for vc in range(2):
                            fns.append(mm(bov[:, vc, csl], g_v[:, jt, vc * 128:(vc + 1) * 128], sm[:, csl], True, False))
                            fns.append(mm(bov[:, vc, csl], sb[:, vc * 128:(vc + 1) * 128], g_qd[d_][:, qsl], False, True))
                        fw.pe_group(fns, ["v", ("sm", d_, smi), ("Sbf", d_, sb_i[d_]), ("qd", d_)], [("ps", bio)])
                        ch = jt * 2 + hf
                        cur = st_i[d_]
                        nxt = 1 - cur
                        stt(g_S32[d_][nxt][:], g_S32[d_][cur][:], g_dec[d_][:, ch:ch + 1], buv[:, hf, :],
                            ALU.mult, ALU.add, [("S32", d_, cur), ("dec", d_), ("ps", biu)], [("S32", d_, nxt)])
                        st_i[d_] = nxt
                        nsb = (sb_i[d_] + 1) % 3
                        copy("act", g_Sbf[d_][nsb][:], g_S32[d_][nxt][:], [("S32", d_, nxt)], [("Sbf", d_, nsb)])
                        sb_i[d_] = nsb
                    if step < 8:
                        copy("act", g_oacc[:, :, tsl], bov, [("ps", bio)], [("oacc", jt)])
                    else:
                        tt_(g_oacc[:, :, tsl], g_oacc[:, :, tsl], bov, ALU.add, [("ps", bio), ("oacc", jt)], [("oacc", jt)])
            fw.barrier()
            for tt in range(NTT):
                sl = slice(tt * TW, (tt + 1) * TW)
                okeys = [("oacc", jt) for jt in range(tt * 4, tt * 4 + 4)]
                act(g_sq2[:], g_oacc[:, :, sl], AF.Square, okeys, ["sq2"])
                bi, b = nbank()
                fw.pe_group([mm(b[:], ones_v[:], g_sq2[:, vc, :], vc == 0, vc == 1) for vc in range(2)],
                            ["sq2", "ones_v"], [("ps", bi)])
                act(g_srt2[:], b[:], AF.Sqrt, [("ps", bi)], ["srt2"], bias=EPS)
                fw.op("dve", lambda e: e.reciprocal(out=g_rstd2[:], in_=g_srt2[:]), ["srt2"], ["rstd2"])
                for vc in range(2):
                    bi, b = nbank()
                    fw.pe_group([mm(b[:], g_wr[:, k, vc * 128:(vc + 1) * 128], hT[:, k, sl], k == 0, k == KC - 1)
                                 for k in range(KC)], hk(ALLK, [tt]) + ["wr"], [("ps", bi)])
                    act(g_sr[vc][:], b[:], AF.Silu, [("ps", bi)], [("sr", vc)])
                    stt(g_t1[:], g_oacc[:, vc, sl], gains[:, HG + j * 2 + vc:HG + j * 2 + vc + 1], g_rstd2[:],
                        ALU.mult, ALU.mult, okeys + ["rstd2", "gains"], ["t1"])
                    tt_(g_go[:, vc, sl], g_t1[:], g_sr[vc][:], ALU.mult, ["t1", ("sr", vc)], [("go", vc, tt)])
                for dc in range(KC):
                    bi, b = nbank()
                    fw.pe_group([mm(b[:], g_wo[:, vc, dc * 128:(dc + 1) * 128], g_go[:, vc, sl], vc == 0, vc == 1)
                                 for vc in range(2)], [("go", 0, tt), ("go", 1, tt), "wo"], [("ps", bi)])
                    tt_(xT[:, dc, sl], xT[:, dc, sl], b[:], ALU.add, xk([dc], [tt]) + [("ps", bi)], xk([dc], [tt]))
            fw.barrier()

    setup_consts()
    for s in range(n_seq):
        if s > 0:
            fw.new_epoch()
        load_x(s)
        fw.barrier()
        for i in layers:
            if i % 2 == 0:
                pool_mixer(i)
            else:
                gla_mixer(i)
            fw.barrier()
            ffn(i)
            fw.barrier()
        final_store(s)
        fw.barrier()

    with nc.Block() as block:
        @block.tensor
        def _(e):
            fw.replay("pe", e)

        @block.scalar
        def _(e):
            fw.replay("act", e)

        @block.vector
        def _(e):
            fw.replay("dve", e)

        @block.gpsimd
        def _(e):
            fw.replay("pool", e)

        @block.sync
        def _(e):
            fw.replay("sp", e)
    stack.close()
    return nc


_WNAMES = ["norm_mix", "norm_ffn", "norm_final", "w_pool", "pool_scale", "w_gla_in", "w_gate_up", "b_gate",
           "gla_head_norm", "w_gla_out", "w_ffn_gate", "w_ffn_up", "w_ffn_down"]


def kernel(**inputs):
    x = np.ascontiguousarray(np.asarray(inputs["x"], dtype=np.float32))
    B = x.shape[0]
    per = B // N_CORES
    nc = build(n_seq=per)
    w = {k: np.ascontiguousarray(np.asarray(inputs[k], dtype=np.float32)) for k in _WNAMES}
    in_maps = []
    for c in range(N_CORES):
        m = {"x": x[c * per:(c + 1) * per]}
        m.update(w)
        in_maps.append(m)
    res = run_bass_kernel_spmd(nc, in_maps, core_ids=list(range(N_CORES)))
    return np.concatenate([np.asarray(r["y"], dtype=np.float32) for r in res.results], axis=0)
```

```python
import numpy as np
from contextlib import ExitStack
import concourse.bass as bass
import concourse.mybir as mybir
from concourse.bass_utils import run_bass_kernel_spmd

F32 = mybir.dt.float32
BF16 = mybir.dt.bfloat16
AF = mybir.ActivationFunctionType
ALU = mybir.AluOpType

S = 2048
D = 1024
KC = 8
DFF = 2816
FC = 22
NTT = 4
TW = 512
DEPTH = 4
EPS = 1e-6
GLA_IN = 3104
N_CORES = 8
FFN_GROUPS = [(0, 4), (4, 4), (8, 4), (12, 4), (16, 3), (19, 3)]


class FW:
    ENG = ("pe", "act", "dve", "pool", "sp")

    def __init__(self, nc, stack):
        self.nc, self.stack = nc, stack
        self.prog = {e: [] for e in self.ENG}
        self.sem, self.cnt = {}, {}
        self.seen = {e: {} for e in self.ENG}
        self.lastw, self.readers = {}, {}
        self.nsem = 0
        self.pe_sems = set()
        self.dma_pool, self.dma_rr = {}, {}
        self.bufinfo, self.overlaps, self.buftok = {}, {}, {}
        self.new_epoch()

    def register(self, t):
        r = t.manual_sbuf_range
        n = t.name
        ov = []
        for o, (a, b) in self.bufinfo.items():
            if a < r[1] and r[0] < b:
                ov.append(o)
                self.overlaps[o].append(n)
        self.bufinfo[n] = (r[0], r[1])
        self.overlaps[n] = ov

    def _phys(self, aps):
        bufs = set()
        for ap in aps:
            n = getattr(getattr(ap, "tensor", None), "name", None)
            if n in self.bufinfo:
                bufs.add(n)
        return bufs

    def _phys_deps(self, d, bufs):
        for b in bufs:
            for o in self.overlaps[b]:
                for t in self.buftok.get(o, {}).values():
                    self._add(d, t)

    def _phys_commit(self, tok, bufs):
        for b in bufs:
            self._add(self.buftok.setdefault(b, {}), tok)

    def _newsem(self, name):
        self.nsem += 1
        return self.stack.enter_context(self.nc.semaphore(f"{name}{self.nsem}"))

    def new_epoch(self):
        for e in ("pe", "act", "dve", "pool"):
            self.sem[e] = self._newsem(e)
            self.cnt[e] = 0
        self.pe_sems.add(id(self.sem["pe"]))

    @staticmethod
    def _add(d, t):
        if t is None:
            return
        s, v = t
        k = id(s)
        if k not in d or d[k][1] < v:
            d[k] = (s, v)

    def _deps(self, reads, writes, e=None):
        d = {}
        for k in reads:
            for t in self.lastw.get(k, {}).values():
                self._add(d, t)
            if isinstance(k, tuple) and k[0] == "ps":
                own = id(self.sem.get(e)) if e in self.sem else None
                for t in self.readers.get(k, {}).values():
                    if id(t[0]) != own:
                        self._add(d, t)
        for k in writes:
            for t in self.lastw.get(k, {}).values():
                self._add(d, t)
            for t in self.readers.get(k, {}).values():
                self._add(d, t)
        return d

    def _waits(self, e, d):
        w = []
        for k, (s, v) in d.items():
            if e == "pe" and k in self.pe_sems:
                continue
            if self.seen[e].get(k, 0) >= v:
                continue
            self.seen[e][k] = v
            w.append((s, v))
        return w

    def _commit(self, tok, reads, writes):
        for k in writes:
            self._add(self.lastw.setdefault(k, {}), tok)
        for k in reads:
            self._add(self.readers.setdefault(k, {}), tok)

    def op(self, e, fn, reads=(), writes=(), aps=()):
        d = self._deps(reads, writes, e)
        bufs = self._phys(aps)
        self._phys_deps(d, bufs)
        w = self._waits(e, d)
        self.cnt[e] += 1
        assert self.cnt[e] < 30000
        tok = (self.sem[e], self.cnt[e])
        self.prog[e].append((w, fn, tok, 1))
        self._commit(tok, reads, writes)
        self._phys_commit(tok, bufs)
        return tok

    def pe_group(self, fns, reads, writes):
        d = self._deps(reads, writes)
        aps = []
        for fn in fns:
            aps += fn.aps
        bufs = self._phys(aps)
        self._phys_deps(d, bufs)
        w = self._waits("pe", d)
        tok = None
        for i, fn in enumerate(fns):
            if i == len(fns) - 1:
                self.cnt["pe"] += 1
                assert self.cnt["pe"] < 30000
                tok = (self.sem["pe"], self.cnt["pe"])
            self.prog["pe"].append((w if i == 0 else [], fn, tok if i == len(fns) - 1 else None, 1))
        self._commit(tok, reads, writes)
        self._phys_commit(tok, bufs)
        return tok

    def dma(self, q, out, in_, reads=(), writes=()):
        NS = 8
        pool = self.dma_pool.setdefault(q, [])
        i = self.dma_rr.get(q, 0)
        self.dma_rr[q] = i + 1
        if len(pool) < NS:
            pool.append([self._newsem("dma" + q), 0])
        ent = pool[i % NS]
        d = self._deps(reads, writes)
        bufs = self._phys([out, in_])
        self._phys_deps(d, bufs)
        if ent[1] > 0:
            self._add(d, (ent[0], ent[1]))
        w = self._waits(q, d)
        ent[1] += 16
        assert ent[1] < 30000
        tok = (ent[0], ent[1])
        self.prog[q].append((w, (lambda eng, o=out, i_=in_: eng.dma_start(out=o, in_=i_)), tok, 16))
        self._commit(tok, reads, writes)
        self._phys_commit(tok, bufs)
        return tok

    def barrier(self):
        toks = [(self.sem[e], self.cnt[e]) for e in ("pe", "act", "dve", "pool") if self.cnt[e] > 0]
        for q, pool in self.dma_pool.items():
            toks += [(s, v) for s, v in pool if v > 0]
        for e in self.ENG:
            d = {}
            for t in toks:
                self._add(d, t)
            w = self._waits(e, d)
            if w:
                self.prog[e].append((w, None, None, 0))

    def replay(self, e, eng):
        for w, fn, tok, inc in self.prog[e]:
            for s, v in w:
                eng.wait_ge(s, v)
            if fn is None:
                continue
            ins = fn(eng)
            if tok is not None:
                ins.then_inc(tok[0], inc)


class Alloc:
    def __init__(self, nc, base, limit):
        self.nc, self.off, self.limit = nc, base, limit

    def t(self, name, shape, dtype):
        n = 1
        for s_ in shape[1:]:
            n *= s_
        nbytes = n * (4 if dtype == F32 else 2)
        off = (self.off + 31) // 32 * 32
        assert off + nbytes <= self.limit, (name, off, nbytes, self.limit)
        self.off = off + nbytes
        return self.nc.alloc_sbuf_tensor_at(name, shape, dtype, offset=off)


def build(n_seq=4, layers=(0, 1, 2, 3)):
    nc = bass.Bass("TRN2", target_bir_lowering=False)
    stack = ExitStack()
    fw = FW(nc, stack)

    def din(name, shape):
        return nc.dram_tensor(name, shape, F32, kind="ExternalInput").ap()

    x_d = din("x", [n_seq, S, D])
    nmix_d = din("norm_mix", [DEPTH, D])
    nffn_d = din("norm_ffn", [DEPTH, D])
    nfin_d = din("norm_final", [D])
    wpool_d = din("w_pool", [2, 4, 256, 256])
    pscale_d = din("pool_scale", [2, D])
    win_d = din("w_gla_in", [2, D, GLA_IN])
    wgu_d = din("w_gate_up", [2, 2, 16, 512])
    bg_d = din("b_gate", [2, 2, 512])
    hg_d = din("gla_head_norm", [2, 256])
    wout_d = din("w_gla_out", [2, D, D])
    wfg_d = din("w_ffn_gate", [DEPTH, D, DFF])
    wfu_d = din("w_ffn_up", [DEPTH, D, DFF])
    wfd_d = din("w_ffn_down", [DEPTH, DFF, D])
    y_d = nc.dram_tensor("y", [n_seq, S, D], F32, kind="ExternalOutput").ap()

    LIMIT = 229376
    A = Alloc(nc, 16512, LIMIT)
    xT = A.t("xT", [128, KC, S], F32)
    hT = A.t("hT", [128, KC, S], BF16)
    ident = A.t("ident", [128, 128], F32)
    ones_d = A.t("ones_d", [128, 128], BF16)
    ones_v = A.t("ones_v", [128, 128], BF16)
    triFf = A.t("triFf", [128, 128], F32)
    triFb = A.t("triFb", [128, 128], F32)
    triTf = A.t("triTf", [128, 128], F32)
    triTb = A.t("triTb", [128, 128], F32)
    mskF = A.t("mskF", [128, 128], F32)
    mskB = A.t("mskB", [128, 128], F32)
    gains = A.t("gains", [128, 128], F32)
    halfm = A.t("halfm", [128, 2], F32)
    pinv = A.t("pinv", [128, 4, 16], F32)
    ARENA = (A.off + 31) // 32 * 32

    GMIX, GFFN, GFIN, PSC, HG = 0, 32, 64, 72, 88

    def at(name, shape, dtype, kib):
        n = 1
        for s_ in shape[1:]:
            n *= s_
        nbytes = n * (4 if dtype == F32 else 2)
        off = ARENA + int(round(kib * 1024))
        assert off % 32 == 0 and off + nbytes <= LIMIT, (name, off, nbytes)
        t = nc.alloc_sbuf_tensor_at(name, shape, dtype, offset=off)
        fw.register(t)
        return t

    ytmp = at("ytmp", [128, KC, TW], F32, 0)
    n_sq = at("n_sq", [128, KC, TW], BF16, 16)
    n_rstd = at("n_rstd", [128, TW], F32, 24)
    stg = at("stg", [128, 128], F32, 26)
    xin = [at(f"xin{i}", [128, D], F32, 53 + 4 * i) for i in range(2)]
    ystg = [at(f"ystg{i}", [128, D], F32, 61 + 4 * i) for i in range(2)]
    f_sq = at("f_sq", [128, KC, TW], BF16, 12)
    f_rstd = at("f_rstd", [128, TW], F32, 36)
    f_wg = [at("f_wg0", [128, KC, 512], BF16, 53), at("f_wg1", [128, KC, 512], BF16, 28)]
    f_wu = [at("f_wu0", [128, KC, 512], BF16, 61), at("f_wu1", [128, KC, 512], BF16, 43)]
    f_wd = [at("f_wd0", [128, 4, D], BF16, 69), at("f_wd1", [128, 4, D], BF16, 93)]
    f_act = at("f_act", [128, 4, S], BF16, 77)
    f_sg = [at(f"f_sg{i}", [128, TW], F32, 101 + 2 * i) for i in range(2)]
    HW_ = S + 16
    p_mix = at("p_mix", [128, 2, S], BF16, 0)
    p_w = at("p_w", [128, 4, 2, 256], BF16, 8)
    p_sq = at("p_sq", [128, KC, TW], BF16, 12)
    p_rstd = at("p_rstd", [128, S], F32, 20)
    p_hA = at("p_hA", [128, HW_], F32, 28)
    p_hB1 = at("p_hB1", [128, HW_], F32, 36.25)
    p_hB2 = at("p_hB2", [128, HW_], F32, 44.5)
    p_fix = at("p_fix", [128, 16], F32, 52.75)
    g_sq = at("g_sq", [128, KC, TW], BF16, 0)
    g_rstd = at("g_rstd", [128, TW], F32, 8)
    g_e = [at(f"g_e{i}", [128, TW], F32, 0 + 2 * i) for i in range(2)]
    g_sp = [at(f"g_sp{i}", [128, 4, 128], F32, 4 + 2 * i) for i in range(2)]
    g_Eq = [at(f"g_Eq{i}", [128, TW], F32, 8 + 2 * i) for i in range(2)]
    g_Ek = [at(f"g_Ek{i}", [128, TW], F32, 12 + 2 * i) for i in range(2)]
    g_Ekd = [at(f"g_Ekd{i}", [128, 4, 128], F32, 16 + 2 * i) for i in range(2)]
    g_sm = [[at(f"g_sm{i}{r}", [128, 128], BF16, 0 + 0.25 * (2 * i + r)) for r in range(2)] for i in range(2)]
    g_S32 = [[at(f"g_S32{i}{r}", [128, 256], F32, 1 + (2 * i + r)) for r in range(2)] for i in range(2)]
    g_Sbf = [[at(f"g_Sbf{i}{r}", [128, 256], BF16, 5 + 0.5 * (3 * i + r)) for r in range(3)] for i in range(2)]
    g_sq2 = at("g_sq2", [128, 2, TW], BF16, 0)
    g_rstd2 = at("g_rstd2", [128, TW], F32, 2)
    g_sr = [at(f"g_sr{i}", [128, TW], F32, 4 + 2 * i) for i in range(2)]
    g_t1 = at("g_t1", [128, TW], F32, 8)
    g_wq = at("g_wq", [128, KC, 128], BF16, 20)
    g_wkv = at("g_wkv", [128, KC, 384], BF16, 22)
    g_wr = at("g_wr", [128, KC, 256], BF16, 28)
    g_wo = at("g_wo", [128, 2, D], BF16, 32)
    g_wgate = at("g_wgate", [128, KC, 32], BF16, 36)
    g_wz = [at(f"g_wz{i}", [33, 512], BF16, 36.5 + i) for i in range(2)]
    g_gT = at("g_gT", [33, S], BF16, 38.5)
    g_qd = [at(f"g_qd{i}", [128, S], BF16, 42.5 + 4 * i) for i in range(2)]
    g_go = at("g_go", [128, 2, S], BF16, 42.5)
    g_ki = [at(f"g_ki{i}", [128, S], BF16, 50.5 + 4 * i) for i in range(2)]
    g_kd = [at(f"g_kd{i}", [128, 16, 128], BF16, 58.5 + 4 * i) for i in range(2)]
    g_v = at("g_v", [128, 16, 256], BF16, 74.5)
    g_oacc = at("g_oacc", [128, 2, S], F32, 82.5)
    g_dec = [at(f"g_dec{i}", [128, 16], F32, 98.5 + 0.125 * i) for i in range(2)]

    banks = [stack.enter_context(nc.psum_tensor(f"psb{i}", [128, 512], F32)) for i in range(8)]
    bank_rr = [0]

    def nbank():
        i = bank_rr[0] % 8
        bank_rr[0] += 1
        return i, banks[i]

    def v3(ap, b):
        return ap.rearrange("p (a b) -> p a b", b=b)

    def mm(out, lhsT, rhs, start, stop):
        f = lambda pe: pe.matmul(out, lhsT=lhsT, rhs=rhs, start=start, stop=stop)
        f.aps = [lhsT, rhs]
        return f

    def tr(out, in_):
        f = lambda pe: pe.transpose(out, in_, ident[:])
        f.aps = [in_]
        return f

    def act(out, in_, func, reads, writes, bias=None, scale=None):
        kw = {}
        if bias is not None:
            kw["bias"] = bias
        if scale is not None:
            kw["scale"] = scale
        return fw.op("act", lambda e: e.activation(out=out, in_=in_, func=func, **kw), reads, writes, [out, in_])

    def copy(eng, out, in_, reads, writes):
        if eng == "act":
            return fw.op("act", lambda e: e.activation(out=out, in_=in_, func=AF.Copy), reads, writes, [out, in_])
        return fw.op(eng, lambda e: e.tensor_copy(out=out, in_=in_), reads, writes, [out, in_])

    def stt(out, in0, scalar, in1, op0, op1, reads, writes, eng="dve"):
        return fw.op(eng, lambda e: e.scalar_tensor_tensor(out=out, in0=in0, scalar=scalar, in1=in1, op0=op0, op1=op1),
                     reads, writes, [out, in0, in1, scalar])

    def tt_(out, in0, in1, op, reads, writes, eng="dve"):
        return fw.op(eng, lambda e: e.tensor_tensor(out=out, in0=in0, in1=in1, op=op), reads, writes, [out, in0, in1])

    def recip(out, in_, reads, writes):
        return fw.op("dve", lambda e: e.reciprocal(out=out, in_=in_), reads, writes, [out, in_])

    def memset(eng, ap, val, writes):
        return fw.op(eng, lambda e: e.memset(ap, val), (), writes, [ap])

    def xk(ks, tts):
        return [("xT", k, t) for k in ks for t in tts]

    def hk(ks, tts):
        return [("hT", k, t) for k in ks for t in tts]

    ALLK = list(range(KC))
    ALLT = list(range(NTT))

    def setup_consts():
        memset("pool", ident[:], 0.0, ["ident"])
        fw.op("pool", lambda e: e.affine_select(out=ident[:], in_=ident[:], pattern=[[-1, 128]],
                                                compare_op=ALU.not_equal, fill=1.0, base=0, channel_multiplier=1),
              ["ident"], ["ident"])
        memset("pool", ones_d[:], 1.0 / D, ["ones_d"])
        memset("pool", ones_v[:], 1.0 / 256, ["ones_v"])
        specs = [
            (triFf, -1.0 / 16, -1, 1, ALU.is_ge),
            (triFb, -1.0 / 16, 1, -1, ALU.is_ge),
            (triTf, -1.0 / 16, 1, -1, ALU.is_gt),
            (triTb, -1.0 / 16, -1, 1, ALU.is_gt),
            (mskF, 1.0, -1, 1, ALU.is_ge),
            (mskB, 1.0, 1, -1, ALU.is_gt),
        ]
        for i, (t, val, cm, pc, cmp_) in enumerate(specs):
            key = ("tri", i)
            memset("pool", t[:], val, [key])
            fw.op("pool", lambda e, t=t, cm=cm, pc=pc, cmp_=cmp_: e.affine_select(
                out=t[:], in_=t[:], pattern=[[pc, 128]], compare_op=cmp_, fill=0.0, base=0, channel_multiplier=cm),
                [key], [key])
        memset("pool", halfm[:], 0.0, ["halfm"])
        memset("pool", halfm[0:64, 0:1], 1.0, ["halfm"])
        memset("pool", halfm[64:128, 1:2], 1.0, ["halfm"])
        memset("pool", pinv[:], 1.0, ["pinv"])
        for g, win in enumerate((2, 4, 8, 16)):
            left = win // 2
            right = win - 1 - left
            for t in range(left):
                memset("pool", pinv[:, g, t:t + 1], 1.0 / (t + right + 1), ["pinv"])
            for r in range(right):
                t = S - right + r
                memset("pool", pinv[:, g, 8 + r:9 + r], 1.0 / (S - t + left), ["pinv"])
        memset("pool", stg[:], 0.0, ["stg"])
        fw.dma("sp", stg[GMIX:GMIX + 32, :], nmix_d.rearrange("i (k p) -> (i k) p", p=128), (), ["stg"])
        fw.dma("sp", stg[GFFN:GFFN + 32, :], nffn_d.rearrange("i (k p) -> (i k) p", p=128), (), ["stg"])
        fw.dma("sp", stg[GFIN:GFIN + 8, :], nfin_d.rearrange("(k p) -> k p", p=128), (), ["stg"])
        fw.dma("sp", stg[PSC:PSC + 16, :], pscale_d.rearrange("i (k p) -> (i k) p", p=128), (), ["stg"])
        fw.dma("sp", stg[HG:HG + 4, :], hg_d.rearrange("i (k p) -> (i k) p", p=128), (), ["stg"])
        bi, b = nbank()
        fw.pe_group([tr(b[:, 0:128], stg[:])], ["stg", "ident"], [("ps", bi)])
        copy("dve", gains[:], b[:, 0:128], [("ps", bi)], ["gains"])

    def load_x_tile(s, j):
        sl = j % 2
        fw.dma("sp", xin[sl][:], x_d[s, j * 128:(j + 1) * 128, :], (), [("xin", sl)])
        for half in range(2):
            bi, b = nbank()
            bv = v3(b[:], 128)
            fw.pe_group([tr(bv[:, kk, :], xin[sl][:, (half * 4 + kk) * 128:(half * 4 + kk + 1) * 128])
                         for kk in range(4)], [("xin", sl), "ident"], [("ps", bi)])
            copy("act" if half == 0 else "dve", xT[:, half * 4:half * 4 + 4, j * 128:(j + 1) * 128], bv,
                 [("ps", bi)], xk(range(half * 4, half * 4 + 4), [j // 4]))

    def rstd_tile(tt, sq, rstd_out, rkey):
        sl = slice(tt * TW, (tt + 1) * TW)
        act(sq[:], xT[:, :, sl], AF.Square, xk(ALLK, [tt]), ["n_sq"])
        bi, b = nbank()
        fw.pe_group([mm(b[:], ones_d[:], sq[:, k, :], k == 0, k == KC - 1) for k in range(KC)],
                    ["n_sq", "ones_d"], [("ps", bi)])
        act(rstd_out, b[:], AF.Sqrt, [("ps", bi)], [rkey], bias=EPS)
        recip(rstd_out, rstd_out, [rkey], [rkey])

    def main_norm(gcol, sq, rstd):
        for tt in range(NTT):
            sl = slice(tt * TW, (tt + 1) * TW)
            rstd_tile(tt, sq, rstd[:], "n_rstd")
            for k in range(KC):
                stt(hT[:, k, sl], xT[:, k, sl], gains[:, gcol + k:gcol + k + 1], rstd[:], ALU.mult, ALU.mult,
                    xk([k], [tt]) + ["n_rstd", "gains"], hk([k], [tt]))

    def final_store_tile(s, tt):
        sl = slice(tt * TW, (tt + 1) * TW)
        rstd_tile(tt, n_sq, n_rstd[:], "n_rstd")
        for k in range(KC):
            stt(ytmp[:, k, :], xT[:, k, sl], gains[:, GFIN + k:GFIN + k + 1], n_rstd[:], ALU.mult, ALU.mult,
                xk([k], [tt]) + ["n_rstd", "gains"], [("ytmp", k)])
        for jj in range(4):
            j = tt * 4 + jj
            sl_ = j % 2
            for half in range(2):
                bi, b = nbank()
                bv = v3(b[:], 128)
                fw.pe_group([tr(bv[:, kk, :], ytmp[:, half * 4 + kk, jj * 128:(jj + 1) * 128]) for kk in range(4)],
                            [("ytmp", half * 4 + kk) for kk in range(4)] + ["ident"], [("ps", bi)])
                copy("act" if half == 0 else "dve", ystg[sl_][:, half * 512:(half + 1) * 512], b[:],
                     [("ps", bi)], [("ystg", sl_, half)])
            fw.dma("sp", y_d[s, j * 128:(j + 1) * 128, :], ystg[sl_][:], [("ystg", sl_, 0), ("ystg", sl_, 1)], ())

    def ffn(i):
        main_norm(GFFN + i * 8, f_sq, f_rstd)
        for gi, (c0, ncn) in enumerate(FFN_GROUPS):
            sl_ = gi % 2
            f0, f1 = c0 * 128, (c0 + ncn) * 128
            fw.dma("pool", f_wg[sl_][:, :, 0:ncn * 128], wfg_d[i].rearrange("(k p) f -> p k f", p=128)[:, :, f0:f1],
                   (), [("wg", sl_)])
            fw.dma("pool", f_wu[sl_][:, :, 0:ncn * 128], wfu_d[i].rearrange("(k p) f -> p k f", p=128)[:, :, f0:f1],
                   (), [("wu", sl_)])
            fw.dma("pool", f_wd[sl_][:, 0:ncn, :], wfd_d[i][f0:f1, :].rearrange("(c p) d -> p c d", p=128),
                   (), [("wd", sl_)])
            for c in range(ncn):
                for tt in range(NTT):
                    sl = slice(tt * TW, (tt + 1) * TW)
                    big, bg = nbank()
                    fw.pe_group([mm(bg[:], f_wg[sl_][:, k, c * 128:(c + 1) * 128], hT[:, k, sl], k == 0, k == KC - 1)
                                 for k in range(KC)], hk(ALLK, [tt]) + [("wg", sl_)], [("ps", big)])
                    biu, bu = nbank()
                    fw.pe_group([mm(bu[:], f_wu[sl_][:, k, c * 128:(c + 1) * 128], hT[:, k, sl], k == 0, k == KC - 1)
                                 for k in range(KC)], hk(ALLK, [tt]) + [("wu", sl_)], [("ps", biu)])
                    sgi = (c * NTT + tt) % 2
                    act(f_sg[sgi][:], bg[:], AF.Silu, [("ps", big)], [("sg", sgi)])
                    tt_(f_act[:, c, sl], f_sg[sgi][:], bu[:], ALU.mult, [("sg", sgi), ("ps", biu)], [("act", c, tt)])
            for tt in range(NTT):
                sl = slice(tt * TW, (tt + 1) * TW)
                for dc in range(KC):
                    bi, b = nbank()
                    fw.pe_group([mm(b[:], f_wd[sl_][:, c, dc * 128:(dc + 1) * 128], f_act[:, c, sl], c == 0, c == ncn - 1)
                                 for c in range(ncn)], [("act", c, tt) for c in range(ncn)] + [("wd", sl_)], [("ps", bi)])
                    tt_(xT[:, dc, sl], xT[:, dc, sl], b[:], ALU.add, xk([dc], [tt]) + [("ps", bi)], xk([dc], [tt]))

    def pool_mixer(i):
        j = i // 2
        fw.dma("pool", p_w[:], wpool_d[j].rearrange("g (c p) d -> p g c d", p=128), (), ["p_w"])
        memset("pool", p_hA[:, 0:8], 0.0, ["hA"])
        memset("pool", p_hA[:, 8 + S:16 + S], 0.0, ["hA"])
        for tt in range(NTT):
            rstd_tile(tt, p_sq, p_rstd[:, tt * TW:(tt + 1) * TW], "p_rstd")
        for g, win in enumerate((2, 4, 8, 16)):
            left = win // 2
            right = win - 1 - left
            for cc in range(2):
                k = 2 * g + cc
                stt(p_hA[:, 8:8 + S], xT[:, k, :], gains[:, GMIX + i * 8 + k:GMIX + i * 8 + k + 1], p_rstd[:],
                    ALU.mult, ALU.mult, xk([k], ALLT) + ["p_rstd", "gains"], ["hA"])
                tt_(p_hB1[:, 1:HW_], p_hA[:, 1:HW_], p_hA[:, 0:HW_ - 1], ALU.add, ["hA"], ["hB1"])
                src = p_hB1
                skey = "hB1"
                if win >= 4:
                    tt_(p_hB2[:, 2:HW_ - 1], p_hB1[:, 3:HW_], p_hB1[:, 1:HW_ - 2], ALU.add, ["hB1"], ["hB2"])
                    src, skey = p_hB2, "hB2"
                if win >= 8:
                    tt_(p_hB1[:, 4:HW_ - 3], p_hB2[:, 6:HW_ - 1], p_hB2[:, 2:HW_ - 5], ALU.add, ["hB2"], ["hB1"])
                    src, skey = p_hB1, "hB1"
                if win >= 16:
                    tt_(p_hB2[:, 8:8 + S], p_hB1[:, 12:12 + S], p_hB1[:, 4:4 + S], ALU.add, ["hB1"], ["hB2"])
                    src, skey = p_hB2, "hB2"
                stt(p_mix[:, cc, :], src[:, 8:8 + S], 1.0 / win, p_hA[:, 8:8 + S], ALU.mult, ALU.subtract,
                    [skey, "hA"], [("mix", cc)])
                tt_(p_fix[:, 0:left], src[:, 8:8 + left], pinv[:, g, 0:left], ALU.mult, [skey, "pinv"], ["p_fix"])
                tt_(p_mix[:, cc, 0:left], p_fix[:, 0:left], p_hA[:, 8:8 + left], ALU.subtract,
                    ["p_fix", "hA"], [("mix", cc)])
                if right > 0:
                    tt_(p_fix[:, 8:8 + right], src[:, 8 + S - right:8 + S], pinv[:, g, 8:8 + right], ALU.mult,
                        [skey, "pinv"], ["p_fix"])
                    tt_(p_mix[:, cc, S - right:S], p_fix[:, 8:8 + right], p_hA[:, 8 + S - right:8 + S], ALU.subtract,
                        ["p_fix", "hA"], [("mix", cc)])
            for tt in range(NTT):
                sl = slice(tt * TW, (tt + 1) * TW)
                for dd in range(2):
                    k = 2 * g + dd
                    bi, b = nbank()
                    fw.pe_group([mm(b[:], p_w[:, g, cc, dd * 128:(dd + 1) * 128], p_mix[:, cc, sl], cc == 0, cc == 1)
                                 for cc in range(2)], [("mix", 0), ("mix", 1), "p_w"], [("ps", bi)])
                    stt(xT[:, k, sl], b[:], gains[:, PSC + j * 8 + k:PSC + j * 8 + k + 1], xT[:, k, sl],
                        ALU.mult, ALU.add, xk([k], [tt]) + [("ps", bi), "gains"], xk([k], [tt]))

    def gla_mixer(i):
        j = i // 2
        main_norm(GMIX + i * 8, g_sq, g_rstd)
        winv = win_d[j].rearrange("(k p) f -> p k f", p=128)
        fw.dma("pool", g_wgate[:], winv[:, :, 3072:3104], (), ["wgate"])
        for d_ in range(2):
            memset("pool", g_wz[d_][0:32, :], 0.0, [("wz", d_)])
            fw.dma("pool", g_wz[d_][16 * d_:16 * d_ + 16, :], wgu_d[j, d_], (), [("wz", d_)])
            fw.dma("pool", g_wz[d_][32:33, :], bg_d[j, d_:d_ + 1, :], (), [("wz", d_)])
        memset("pool", g_gT[32:33, :], 1.0, ["gT"])
        for tt in range(NTT):
            sl = slice(tt * TW, (tt + 1) * TW)
            bi, b = nbank()
            fw.pe_group([mm(b[0:32, :], g_wgate[:, k, :], hT[:, k, sl], k == 0, k == KC - 1) for k in range(KC)],
                        hk(ALLK, [tt]) + ["wgate"], [("ps", bi)])
            copy("act", g_gT[0:32, sl], b[0:32, :], [("ps", bi)], ["gT"])

        tris_f = (triFf, triFb)
        tris_t = (triTf, triTb)
        msks = (mskF, mskB)
        for h in range(4):
            fw.dma("pool", g_wq[:], winv[:, :, h * 128:(h + 1) * 128], (), ["wq"])
            fw.dma("pool", g_wkv[:, :, 0:128], winv[:, :, 512 + h * 128:512 + (h + 1) * 128], (), ["wkv"])
            fw.dma("pool", g_wkv[:, :, 128:384], winv[:, :, 1024 + h * 256:1024 + (h + 1) * 256], (), ["wkv"])
            fw.dma("pool", g_wr[:], winv[:, :, 2048 + h * 256:2048 + (h + 1) * 256], (), ["wr"])
            fw.dma("pool", g_wo[:], wout_d[j][h * 256:(h + 1) * 256, :].rearrange("(c p) d -> p c d", p=128), (), ["wo"])
            for tt in range(NTT):
                sl = slice(tt * TW, (tt + 1) * TW)
                zb = []
                for d_ in range(2):
                    bi, b = nbank()
                    bv = v3(b[:], 128)
                    for jj in range(4):
                        tsl = slice(tt * TW + jj * 128, tt * TW + (jj + 1) * 128)
                        fw.pe_group([mm(bv[:, jj, :], g_gT[0:33, tsl], g_wz[d_][0:33, h * 128:(h + 1) * 128], True, True)],
                                    ["gT", ("wz", d_)], [("ps", bi)])
                    zb.append((bi, b))
                for d_ in range(2):
                    bi, b = zb[d_]
                    act(g_e[d_][:], b[:], AF.Exp, [("ps", bi)], [("e", d_)], scale=-1.0)
                    act(g_sp[d_][:].rearrange("p a b -> p (a b)"), g_e[d_][:], AF.Ln, [("e", d_)], [("sp", d_)], bias=1.0)
                biq, bq = nbank()
                fw.pe_group([mm(bq[:], g_wq[:, k, :], hT[:, k, sl], k == 0, k == KC - 1) for k in range(KC)],
                            hk(ALLK, [tt]) + ["wq"], [("ps", biq)])
                bik, bk = nbank()
                fw.pe_group([mm(bk[:], g_wkv[:, k, 0:128], hT[:, k, sl], k == 0, k == KC - 1) for k in range(KC)],
                            hk(ALLK, [tt]) + ["wkv"], [("ps", bik)])
                cb = []
                for d_ in range(2):
                    bi, b = nbank()
                    bv = v3(b[:], 128)
                    for jj in range(4):
                        fw.pe_group([mm(bv[:, jj, :], g_sp[d_][:, jj, :], tris_f[d_][:], True, True)],
                                    [("sp", d_), ("tri", d_)], [("ps", bi)])
                    bi2, b2 = nbank()
                    fw.pe_group([mm(b2[:], tris_t[d_][:], g_sp[d_][:].rearrange("p a b -> p (a b)"), True, True)],
                                [("sp", d_), ("tri", 2 + d_)], [("ps", bi2)])
                    cb.append((bi, b, bi2, b2))
                for d_ in range(2):
                    bi, b, bi2, b2 = cb[d_]
                    act(g_Eq[d_][:], b[:], AF.Exp, [("ps", bi)], [("Eq", d_)])
                    act(g_Ek[d_][:], b[:], AF.Exp, [("ps", bi)], [("Ek", d_)], scale=-1.0)
                    act(g_Ekd[d_][:].rearrange("p a b -> p (a b)"), b2[:], AF.Exp, [("ps", bi2)], [("Ekd", d_)])
                for d_ in range(2):
                    stt(g_qd[d_][:, sl], bq[:], 128.0 ** -0.5, g_Eq[d_][:], ALU.mult, ALU.mult,
                        [("ps", biq), ("Eq", d_)], [("qd", d_)])
                for d_ in range(2):
                    tt_(g_ki[d_][:, sl], bk[:], g_Ek[d_][:], ALU.mult, [("ps", bik), ("Ek", d_)], [("ki", d_)])
                for d_ in range(2):
                    col = 127 if d_ == 0 else 0
                    copy("dve", g_dec[d_][:, tt * 4:(tt + 1) * 4], v3(g_Eq[d_][:], 128)[:, :, col], [("Eq", d_)], [("dec", d_)])
                for jj in range(4):
                    jt = tt * 4 + jj
                    tsl = slice(jt * 128, (jt + 1) * 128)
                    bi, b = nbank()
                    fw.pe_group([mm(b[:, 0:128], hT[:, k, tsl], g_wkv[:, k, 0:128], k == 0, k == KC - 1) for k in range(KC)],
                                hk(ALLK, [tt]) + ["wkv"], [("ps", bi)])
                    for d_ in range(2):
                        tt_(g_kd[d_][:, jt, :], b[:, 0:128], g_Ekd[d_][:, jj, :], ALU.mult,
                            [("ps", bi), ("Ekd", d_)], [("kd", d_)])
                    bi, b = nbank()
                    fw.pe_group([mm(b[:, 0:256], hT[:, k, tsl], g_wkv[:, k, 128:384], k == 0, k == KC - 1) for k in range(KC)],
                                hk(ALLK, [tt]) + ["wkv"], [("ps", bi)])
                    copy("act", g_v[:, jt, :], b[:, 0:256], [("ps", bi)], ["v"])
            for d_ in range(2):
                memset("dve", g_S32[d_][0][:], 0.0, [("S32", d_, 0)])
                memset("dve", g_Sbf[d_][0][:], 0.0, [("Sbf", d_, 0)])
            st_i = [0, 0]
            sb_i = [0, 0]
            for step in range(16):
                pend = []
                for d_ in range(2):
                    jt = step if d_ == 0 else 15 - step
                    tsl = slice(jt * 128, (jt + 1) * 128)
                    bi, b = nbank()
                    fw.pe_group([mm(b[:, 0:128], g_ki[d_][:, tsl], g_qd[d_][:, tsl], True, True)],
                                [("ki", d_), ("qd", d_)], [("ps", bi)])
                    biu, bu = nbank()
                    fw.pe_group([mm(bu[:, 0:256], g_kd[d_][:, jt, :], g_v[:, jt, :], True, True)],
                                [("kd", d_), "v"], [("ps", biu)])
                    smi = step % 2
                    sm = g_sm[d_][smi]
                    tt_(sm[:], b[:, 0:128], msks[d_][:], ALU.mult, [("ps", bi), ("tri", 4 + d_)], [("sm", d_, smi)])
                    pend.append((d_, jt, tsl, biu, bu, smi, sm))
                for d_, jt, tsl, biu, bu, smi, sm in pend:
                    bio, bo = nbank()
                    bov = v3(bo[:, 0:256], 128)
                    sb = g_Sbf[d_][sb_i[d_]]
                    fns = []
                    for vc in range(2):
                        fns.append(mm(bov[:, vc, :], g_v[:, jt, vc * 128:(vc + 1) * 128], sm[:], True, False))
                        fns.append(mm(bov[:, vc, :], sb[:, vc * 128:(vc + 1) * 128], g_qd[d_][:, tsl], False, True))
                    fw.pe_group(fns, ["v", ("sm", d_, smi), ("Sbf", d_, sb_i[d_]), ("qd", d_)], [("ps", bio)])
                    cur = st_i[d_]
                    nxt = 1 - cur
                    stt(g_S32[d_][nxt][:], g_S32[d_][cur][:], g_dec[d_][:, jt:jt + 1], bu[:, 0:256],
                        ALU.mult, ALU.add, [("S32", d_, cur), ("dec", d_), ("ps", biu)], [("S32", d_, nxt)])
                    st_i[d_] = nxt
                    nsb = (sb_i[d_] + 1) % 3
                    copy("act", g_Sbf[d_][nsb][:], g_S32[d_][nxt][:], [("S32", d_, nxt)], [("Sbf", d_, nsb)])
                    sb_i[d_] = nsb
                    if step < 8:
                        copy("act", g_oacc[:, :, tsl], bov, [("ps", bio)], [("oacc", jt)])
                    else:
                        tt_(g_oacc[:, :, tsl], g_oacc[:, :, tsl], bov, ALU.add, [("ps", bio), ("oacc", jt)], [("oacc", jt)])
            def p4a(tt):
                sl = slice(tt * TW, (tt + 1) * TW)
                okeys = [("oacc", jt) for jt in range(tt * 4, tt * 4 + 4)]
                act(g_sq2[:], g_oacc[:, :, sl], AF.Square, okeys, ["sq2"])
                bi, b = nbank()
                fw.pe_group([mm(b[:], ones_v[:], g_sq2[:, vc, :], vc == 0, vc == 1) for vc in range(2)],
                            ["sq2", "ones_v"], [("ps", bi)])
                act(g_rstd2[:], b[:], AF.Sqrt, [("ps", bi)], ["rstd2"], bias=EPS)
                recip(g_rstd2[:], g_rstd2[:], ["rstd2"], ["rstd2"])
                for vc in range(2):
                    bi, b = nbank()
                    fw.pe_group([mm(b[:], g_wr[:, k, vc * 128:(vc + 1) * 128], hT[:, k, sl], k == 0, k == KC - 1)
                                 for k in range(KC)], hk(ALLK, [tt]) + ["wr"], [("ps", bi)])
                    act(g_sr[vc][:], b[:], AF.Silu, [("ps", bi)], [("sr", vc)])
                    stt(g_t1[:], g_oacc[:, vc, sl], gains[:, HG + j * 2 + vc:HG + j * 2 + vc + 1], g_rstd2[:],
                        ALU.mult, ALU.mult, okeys + ["rstd2", "gains"], ["t1"])
                    tt_(g_go[:, vc, sl], g_t1[:], g_sr[vc][:], ALU.mult, ["t1", ("sr", vc)], [("go", vc, tt)])

            def p4b(tt):
                sl = slice(tt * TW, (tt + 1) * TW)
                for dc in range(KC):
                    bi, b = nbank()
                    fw.pe_group([mm(b[:], g_wo[:, vc, dc * 128:(dc + 1) * 128], g_go[:, vc, sl], vc == 0, vc == 1)
                                 for vc in range(2)], [("go", 0, tt), ("go", 1, tt), "wo"], [("ps", bi)])
                    tt_(xT[:, dc, sl], xT[:, dc, sl], b[:], ALU.add, xk([dc], [tt]) + [("ps", bi)], xk([dc], [tt]))

            p4a(0)
            for tt in range(1, NTT):
                p4a(tt)
                p4b(tt - 1)
            p4b(NTT - 1)

    setup_consts()
    for j in range(16):
        load_x_tile(0, j)
    for s in range(n_seq):
        if s > 0:
            fw.new_epoch()
        for i in layers:
            if i % 2 == 0:
                pool_mixer(i)
            else:
                gla_mixer(i)
            ffn(i)
        for tt in range(NTT):
            final_store_tile(s, tt)
            if s + 1 < n_seq:
                for j in range(4 * tt, 4 * tt + 4):
                    load_x_tile(s + 1, j)
    fw.barrier()

    with nc.Block() as block:
        @block.tensor
        def _(e):
            fw.replay("pe", e)

        @block.scalar
        def _(e):
            fw.replay("act", e)

        @block.vector
        def _(e):
            fw.replay("dve", e)

        @block.gpsimd
        def _(e):
            fw.replay("pool", e)

        @block.sync
        def _(e):
            fw.replay("sp", e)
    stack.close()
    return nc


_WNAMES = ["norm_mix", "norm_ffn", "norm_final", "w_pool", "pool_scale", "w_gla_in", "w_gate_up", "b_gate",
           "gla_head_norm", "w_gla_out", "w_ffn_gate", "w_ffn_up", "w_ffn_down"]


def kernel(**inputs):
    x = np.ascontiguousarray(np.asarray(inputs["x"], dtype=np.float32))
    B = x.shape[0]
    per = B // N_CORES
    nc = build(n_seq=per)
    w = {k: np.ascontiguousarray(np.asarray(inputs[k], dtype=np.float32)) for k in _WNAMES}
    in_maps = []
    for c in range(N_CORES):
        m = {"x": x[c * per:(c + 1) * per]}
        m.update(w)
        in_maps.append(m)
    res = run_bass_kernel_spmd(nc, in_maps, core_ids=list(range(N_CORES)))
    return np.concatenate([np.asarray(r["y"], dtype=np.float32) for r in res.results], axis=0)
```

```python
import numpy as np
from contextlib import ExitStack
import concourse.bass as bass
import concourse.mybir as mybir
from concourse.bass_utils import run_bass_kernel_spmd

F32 = mybir.dt.float32
BF16 = mybir.dt.bfloat16
AF = mybir.ActivationFunctionType
ALU = mybir.AluOpType

S = 2048
D = 1024
KC = 8
DFF = 2816
FC = 22
NTT = 4
TW = 512
DEPTH = 4
EPS = 1e-6
GLA_IN = 3104
N_CORES = 8
FFN_GROUPS = [(0, 4), (4, 4), (8, 4), (12, 4), (16, 3), (19, 3)]


class FW:
    ENG = ("pe", "act", "dve", "pool", "sp")

    def __init__(self, nc, stack):
        self.nc, self.stack = nc, stack
        self.prog = {e: [] for e in self.ENG}
        self.sem, self.cnt = {}, {}
        self.seen = {e: {} for e in self.ENG}
        self.lastw, self.readers = {}, {}
        self.nsem = 0
        self.pe_sems = set()
        self.dma_pool, self.dma_rr = {}, {}
        self.bufinfo, self.overlaps, self.buftok = {}, {}, {}
        self.new_epoch()

    def register(self, t):
        r = t.manual_sbuf_range
        n = t.name
        ov = []
        for o, (a, b) in self.bufinfo.items():
            if a < r[1] and r[0] < b:
                ov.append(o)
                self.overlaps[o].append(n)
        self.bufinfo[n] = (r[0], r[1])
        self.overlaps[n] = ov

    def _phys(self, aps):
        bufs = set()
        for ap in aps:
            n = getattr(getattr(ap, "tensor", None), "name", None)
            if n in self.bufinfo:
                bufs.add(n)
        return bufs

    def _phys_deps(self, d, bufs):
        for b in bufs:
            for o in self.overlaps[b]:
                for t in self.buftok.get(o, {}).values():
                    self._add(d, t)

    def _phys_commit(self, tok, bufs):
        for b in bufs:
            self._add(self.buftok.setdefault(b, {}), tok)

    def _newsem(self, name):
        self.nsem += 1
        return self.stack.enter_context(self.nc.semaphore(f"{name}{self.nsem}"))

    def new_epoch(self):
        for e in ("pe", "act", "dve", "pool"):
            self.sem[e] = self._newsem(e)
            self.cnt[e] = 0
        self.pe_sems.add(id(self.sem["pe"]))

    @staticmethod
    def _add(d, t):
        if t is None:
            return
        s, v = t
        k = id(s)
        if k not in d or d[k][1] < v:
            d[k] = (s, v)

    def _deps(self, reads, writes, e=None):
        d = {}
        for k in reads:
            for t in self.lastw.get(k, {}).values():
                self._add(d, t)
            if isinstance(k, tuple) and k[0] == "ps":
                own = id(self.sem.get(e)) if e in self.sem else None
                for t in self.readers.get(k, {}).values():
                    if id(t[0]) != own:
                        self._add(d, t)
        for k in writes:
            for t in self.lastw.get(k, {}).values():
                self._add(d, t)
            for t in self.readers.get(k, {}).values():
                self._add(d, t)
        return d

    def _waits(self, e, d):
        w = []
        for k, (s, v) in d.items():
            if e == "pe" and k in self.pe_sems:
                continue
            if self.seen[e].get(k, 0) >= v:
                continue
            self.seen[e][k] = v
            w.append((s, v))
        return w

    def _commit(self, tok, reads, writes):
        for k in writes:
            self._add(self.lastw.setdefault(k, {}), tok)
        for k in reads:
            self._add(self.readers.setdefault(k, {}), tok)

    def op(self, e, fn, reads=(), writes=(), aps=()):
        d = self._deps(reads, writes, e)
        bufs = self._phys(aps)
        self._phys_deps(d, bufs)
        w = self._waits(e, d)
        self.cnt[e] += 1
        assert self.cnt[e] < 30000
        tok = (self.sem[e], self.cnt[e])
        self.prog[e].append((w, fn, tok, 1))
        self._commit(tok, reads, writes)
        self._phys_commit(tok, bufs)
        return tok

    def pe_group(self, fns, reads, writes):
        d = self._deps(reads, writes)
        aps = []
        for fn in fns:
            aps += fn.aps
        bufs = self._phys(aps)
        self._phys_deps(d, bufs)
        w = self._waits("pe", d)
        tok = None
        for i, fn in enumerate(fns):
            if i == len(fns) - 1:
                self.cnt["pe"] += 1
                assert self.cnt["pe"] < 30000
                tok = (self.sem["pe"], self.cnt["pe"])
            self.prog["pe"].append((w if i == 0 else [], fn, tok if i == len(fns) - 1 else None, 1))
        self._commit(tok, reads, writes)
        self._phys_commit(tok, bufs)
        return tok

    def dma(self, q, out, in_, reads=(), writes=()):
        NS = 8
        pool = self.dma_pool.setdefault(q, [])
        i = self.dma_rr.get(q, 0)
        self.dma_rr[q] = i + 1
        if len(pool) < NS:
            pool.append([self._newsem("dma" + q), 0])
        ent = pool[i % NS]
        d = self._deps(reads, writes)
        bufs = self._phys([out, in_])
        self._phys_deps(d, bufs)
        if ent[1] > 0:
            self._add(d, (ent[0], ent[1]))
        w = self._waits(q, d)
        ent[1] += 16
        assert ent[1] < 30000
        tok = (ent[0], ent[1])
        self.prog[q].append((w, (lambda eng, o=out, i_=in_: eng.dma_start(out=o, in_=i_)), tok, 16))
        self._commit(tok, reads, writes)
        self._phys_commit(tok, bufs)
        return tok

    def barrier(self):
        toks = [(self.sem[e], self.cnt[e]) for e in ("pe", "act", "dve", "pool") if self.cnt[e] > 0]
        for q, pool in self.dma_pool.items():
            toks += [(s, v) for s, v in pool if v > 0]
        for e in self.ENG:
            d = {}
            for t in toks:
                self._add(d, t)
            w = self._waits(e, d)
            if w:
                self.prog[e].append((w, None, None, 0))

    def replay(self, e, eng):
        for w, fn, tok, inc in self.prog[e]:
            for s, v in w:
                eng.wait_ge(s, v)
            if fn is None:
                continue
            ins = fn(eng)
            if tok is not None:
                ins.then_inc(tok[0], inc)


class Alloc:
    def __init__(self, nc, base, limit):
        self.nc, self.off, self.limit = nc, base, limit

    def t(self, name, shape, dtype):
        n = 1
        for s_ in shape[1:]:
            n *= s_
        nbytes = n * (4 if dtype == F32 else 2)
        off = (self.off + 31) // 32 * 32
        assert off + nbytes <= self.limit, (name, off, nbytes, self.limit)
        self.off = off + nbytes
        return self.nc.alloc_sbuf_tensor_at(name, shape, dtype, offset=off)


def build(n_seq=4, layers=(0, 1, 2, 3)):
    nc = bass.Bass("TRN2", target_bir_lowering=False)
    stack = ExitStack()
    fw = FW(nc, stack)

    def din(name, shape):
        return nc.dram_tensor(name, shape, F32, kind="ExternalInput").ap()

    x_d = din("x", [n_seq, S, D])
    nmix_d = din("norm_mix", [DEPTH, D])
    nffn_d = din("norm_ffn", [DEPTH, D])
    nfin_d = din("norm_final", [D])
    wpool_d = din("w_pool", [2, 4, 256, 256])
    pscale_d = din("pool_scale", [2, D])
    win_d = din("w_gla_in", [2, D, GLA_IN])
    wgu_d = din("w_gate_up", [2, 2, 16, 512])
    bg_d = din("b_gate", [2, 2, 512])
    hg_d = din("gla_head_norm", [2, 256])
    wout_d = din("w_gla_out", [2, D, D])
    wfg_d = din("w_ffn_gate", [DEPTH, D, DFF])
    wfu_d = din("w_ffn_up", [DEPTH, D, DFF])
    wfd_d = din("w_ffn_down", [DEPTH, DFF, D])
    y_d = nc.dram_tensor("y", [n_seq, S, D], F32, kind="ExternalOutput").ap()

    LIMIT = 229376
    A = Alloc(nc, 16512, LIMIT)
    xT = A.t("xT", [128, KC, S], F32)
    hT = A.t("hT", [128, KC, S], BF16)
    ident = A.t("ident", [128, 128], F32)
    ones_d = A.t("ones_d", [128, 128], BF16)
    ones_v = A.t("ones_v", [128, 128], BF16)
    triFf = A.t("triFf", [128, 128], F32)
    triFb = A.t("triFb", [128, 128], F32)
    triTf = A.t("triTf", [128, 128], F32)
    triTb = A.t("triTb", [128, 128], F32)
    mskF = A.t("mskF", [128, 128], F32)
    mskB = A.t("mskB", [128, 128], F32)
    gains = A.t("gains", [128, 128], F32)
    halfm = A.t("halfm", [128, 2], F32)
    pinv = A.t("pinv", [128, 4, 16], F32)
    ARENA = (A.off + 31) // 32 * 32

    GMIX, GFFN, GFIN, PSC, HG = 0, 32, 64, 72, 88

    def at(name, shape, dtype, kib):
        n = 1
        for s_ in shape[1:]:
            n *= s_
        nbytes = n * (4 if dtype == F32 else 2)
        off = ARENA + int(round(kib * 1024))
        assert off % 32 == 0 and off + nbytes <= LIMIT, (name, off, nbytes)
        t = nc.alloc_sbuf_tensor_at(name, shape, dtype, offset=off)
        fw.register(t)
        return t

    ytmp = at("ytmp", [128, KC, TW], F32, 0)
    n_sq = at("n_sq", [128, KC, TW], BF16, 16)
    n_rstd = at("n_rstd", [128, TW], F32, 24)
    stg = at("stg", [128, 128], F32, 26)
    xin = [at(f"xin{i}", [128, D], F32, 53 + 4 * i) for i in range(2)]
    ystg = [at(f"ystg{i}", [128, D], F32, 61 + 4 * i) for i in range(2)]
    f_sq = at("f_sq", [128, KC, TW], BF16, 12)
    f_rstd = at("f_rstd", [128, TW], F32, 36)
    f_wg = [at("f_wg0", [128, KC, 512], BF16, 53), at("f_wg1", [128, KC, 512], BF16, 28)]
    f_wu = [at("f_wu0", [128, KC, 512], BF16, 61), at("f_wu1", [128, KC, 512], BF16, 43)]
    f_wd = [at("f_wd0", [128, 4, D], BF16, 69), at("f_wd1", [128, 4, D], BF16, 93)]
    f_act = at("f_act", [128, 4, S], BF16, 77)
    f_sg = [at(f"f_sg{i}", [128, TW], F32, 101 + 2 * i) for i in range(2)]
    HW_ = S + 16
    p_mix = at("p_mix", [128, 2, S], BF16, 0)
    p_w = at("p_w", [128, 4, 2, 256], BF16, 8)
    p_sq = at("p_sq", [128, KC, TW], BF16, 12)
    p_rstd = at("p_rstd", [128, S], F32, 20)
    p_hA = at("p_hA", [128, HW_], F32, 28)
    p_hB1 = at("p_hB1", [128, HW_], F32, 36.25)
    p_hB2 = at("p_hB2", [128, HW_], F32, 44.5)
    p_fix = at("p_fix", [128, 16], F32, 52.75)
    g_sq = at("g_sq", [128, KC, TW], BF16, 0)
    g_rstd = at("g_rstd", [128, TW], F32, 8)
    g_e = [at(f"g_e{i}", [128, TW], F32, 0 + 2 * i) for i in range(2)]
    g_sp = [at(f"g_sp{i}", [128, 4, 128], F32, 4 + 2 * i) for i in range(2)]
    g_Eq = [at(f"g_Eq{i}", [128, TW], F32, 8 + 2 * i) for i in range(2)]
    g_Ek = [at(f"g_Ek{i}", [128, TW], F32, 12 + 2 * i) for i in range(2)]
    g_Ekd = [at(f"g_Ekd{i}", [128, 4, 128], F32, 16 + 2 * i) for i in range(2)]
    g_sm = [[at(f"g_sm{i}{r}", [128, 128], BF16, 0 + 0.25 * (2 * i + r)) for r in range(2)] for i in range(2)]
    g_S32 = [[at(f"g_S32{i}{r}", [128, 256], F32, 1 + (2 * i + r)) for r in range(2)] for i in range(2)]
    g_Sbf = [[at(f"g_Sbf{i}{r}", [128, 256], BF16, 5 + 0.5 * (3 * i + r)) for r in range(3)] for i in range(2)]
    g_sq2 = at("g_sq2", [128, 2, TW], BF16, 0)
    g_rstd2 = at("g_rstd2", [128, TW], F32, 2)
    g_sr = [at(f"g_sr{i}", [128, TW], F32, 4 + 2 * i) for i in range(2)]
    g_t1 = at("g_t1", [128, TW], F32, 8)
    g_wq = at("g_wq", [128, KC, 128], BF16, 20)
    g_wkv = at("g_wkv", [128, KC, 384], BF16, 22)
    g_wr = at("g_wr", [128, KC, 256], BF16, 28)
    g_wo = at("g_wo", [128, 2, D], BF16, 32)
    g_wgate = at("g_wgate", [128, KC, 32], BF16, 36)
    g_wz = [at(f"g_wz{i}", [33, 512], BF16, 36.5 + i) for i in range(2)]
    g_gT = at("g_gT", [33, S], BF16, 38.5)
    g_qd = [at(f"g_qd{i}", [128, S], BF16, 42.5 + 4 * i) for i in range(2)]
    g_go = at("g_go", [128, 2, S], BF16, 42.5)
    g_ki = [at(f"g_ki{i}", [128, S], BF16, 50.5 + 4 * i) for i in range(2)]
    g_kd = [at(f"g_kd{i}", [128, 16, 128], BF16, 58.5 + 4 * i) for i in range(2)]
    g_v = at("g_v", [128, 16, 256], BF16, 74.5)
    g_oacc = at("g_oacc", [128, 2, S], F32, 82.5)
    g_dec = [at(f"g_dec{i}", [128, 16], F32, 98.5 + 0.125 * i) for i in range(2)]

    banks = [stack.enter_context(nc.psum_tensor(f"psb{i}", [128, 512], F32)) for i in range(8)]
    bank_rr = [0]

    def nbank():
        i = bank_rr[0] % 8
        bank_rr[0] += 1
        return i, banks[i]

    def v3(ap, b):
        return ap.rearrange("p (a b) -> p a b", b=b)

    def mm(out, lhsT, rhs, start, stop):
        f = lambda pe: pe.matmul(out, lhsT=lhsT, rhs=rhs, start=start, stop=stop)
        f.aps = [lhsT, rhs]
        return f

    def tr(out, in_):
        f = lambda pe: pe.transpose(out, in_, ident[:])
        f.aps = [in_]
        return f

    def act(out, in_, func, reads, writes, bias=None, scale=None):
        kw = {}
        if bias is not None:
            kw["bias"] = bias
        if scale is not None:
            kw["scale"] = scale
        return fw.op("act", lambda e: e.activation(out=out, in_=in_, func=func, **kw), reads, writes, [out, in_])

    def copy(eng, out, in_, reads, writes):
        if eng == "act":
            return fw.op("act", lambda e: e.activation(out=out, in_=in_, func=AF.Copy), reads, writes, [out, in_])
        return fw.op(eng, lambda e: e.tensor_copy(out=out, in_=in_), reads, writes, [out, in_])

    def stt(out, in0, scalar, in1, op0, op1, reads, writes, eng="dve"):
        return fw.op(eng, lambda e: e.scalar_tensor_tensor(out=out, in0=in0, scalar=scalar, in1=in1, op0=op0, op1=op1),
                     reads, writes, [out, in0, in1, scalar])

    def tt_(out, in0, in1, op, reads, writes, eng="dve"):
        return fw.op(eng, lambda e: e.tensor_tensor(out=out, in0=in0, in1=in1, op=op), reads, writes, [out, in0, in1])

    def recip(out, in_, reads, writes):
        return fw.op("dve", lambda e: e.reciprocal(out=out, in_=in_), reads, writes, [out, in_])

    def memset(eng, ap, val, writes):
        return fw.op(eng, lambda e: e.memset(ap, val), (), writes, [ap])

    def xk(ks, tts):
        return [("xT", k, t) for k in ks for t in tts]

    def hk(ks, tts):
        return [("hT", k, t) for k in ks for t in tts]

    ALLK = list(range(KC))
    ALLT = list(range(NTT))

    def setup_consts():
        memset("pool", ident[:], 0.0, ["ident"])
        fw.op("pool", lambda e: e.affine_select(out=ident[:], in_=ident[:], pattern=[[-1, 128]],
                                                compare_op=ALU.not_equal, fill=1.0, base=0, channel_multiplier=1),
              ["ident"], ["ident"])
        memset("pool", ones_d[:], 1.0 / D, ["ones_d"])
        memset("pool", ones_v[:], 1.0 / 256, ["ones_v"])
        specs = [
            (triFf, -1.0 / 16, -1, 1, ALU.is_ge),
            (triFb, -1.0 / 16, 1, -1, ALU.is_ge),
            (triTf, -1.0 / 16, 1, -1, ALU.is_gt),
            (triTb, -1.0 / 16, -1, 1, ALU.is_gt),
            (mskF, 1.0, -1, 1, ALU.is_ge),
            (mskB, 1.0, 1, -1, ALU.is_gt),
        ]
        for i, (t, val, cm, pc, cmp_) in enumerate(specs):
            key = ("tri", i)
            memset("pool", t[:], val, [key])
            fw.op("pool", lambda e, t=t, cm=cm, pc=pc, cmp_=cmp_: e.affine_select(
                out=t[:], in_=t[:], pattern=[[pc, 128]], compare_op=cmp_, fill=0.0, base=0, channel_multiplier=cm),
                [key], [key])
        memset("pool", halfm[:], 0.0, ["halfm"])
        memset("pool", halfm[0:64, 0:1], 1.0, ["halfm"])
        memset("pool", halfm[64:128, 1:2], 1.0, ["halfm"])
        memset("pool", pinv[:], 1.0, ["pinv"])
        for g, win in enumerate((2, 4, 8, 16)):
            left = win // 2
            right = win - 1 - left
            for t in range(left):
                memset("pool", pinv[:, g, t:t + 1], 1.0 / (t + right + 1), ["pinv"])
            for r in range(right):
                t = S - right + r
                memset("pool", pinv[:, g, 8 + r:9 + r], 1.0 / (S - t + left), ["pinv"])
        memset("pool", stg[:], 0.0, ["stg"])
        fw.dma("sp", stg[GMIX:GMIX + 32, :], nmix_d.rearrange("i (k p) -> (i k) p", p=128), (), ["stg"])
        fw.dma("sp", stg[GFFN:GFFN + 32, :], nffn_d.rearrange("i (k p) -> (i k) p", p=128), (), ["stg"])
        fw.dma("sp", stg[GFIN:GFIN + 8, :], nfin_d.rearrange("(k p) -> k p", p=128), (), ["stg"])
        fw.dma("sp", stg[PSC:PSC + 16, :], pscale_d.rearrange("i (k p) -> (i k) p", p=128), (), ["stg"])
        fw.dma("sp", stg[HG:HG + 4, :], hg_d.rearrange("i (k p) -> (i k) p", p=128), (), ["stg"])
        bi, b = nbank()
        fw.pe_group([tr(b[:, 0:128], stg[:])], ["stg", "ident"], [("ps", bi)])
        copy("dve", gains[:], b[:, 0:128], [("ps", bi)], ["gains"])

    def load_x_tile(s, j):
        sl = j % 2
        fw.dma("sp", xin[sl][:], x_d[s, j * 128:(j + 1) * 128, :], (), [("xin", sl)])
        for half in range(2):
            bi, b = nbank()
            bv = v3(b[:], 128)
            fw.pe_group([tr(bv[:, kk, :], xin[sl][:, (half * 4 + kk) * 128:(half * 4 + kk + 1) * 128])
                         for kk in range(4)], [("xin", sl), "ident"], [("ps", bi)])
            copy("act" if half == 0 else "dve", xT[:, half * 4:half * 4 + 4, j * 128:(j + 1) * 128], bv,
                 [("ps", bi)], xk(range(half * 4, half * 4 + 4), [j // 4]))

    def rstd_tile(tt, sq, rstd_out, rkey):
        sl = slice(tt * TW, (tt + 1) * TW)
        act(sq[:], xT[:, :, sl], AF.Square, xk(ALLK, [tt]), ["n_sq"])
        bi, b = nbank()
        fw.pe_group([mm(b[:], ones_d[:], sq[:, k, :], k == 0, k == KC - 1) for k in range(KC)],
                    ["n_sq", "ones_d"], [("ps", bi)])
        act(rstd_out, b[:], AF.Ln, [("ps", bi)], [rkey], bias=EPS)
        act(rstd_out, rstd_out, AF.Exp, [rkey], [rkey], scale=-0.5)

    def main_norm(gcol, sq, rstd):
        for tt in range(NTT):
            sl = slice(tt * TW, (tt + 1) * TW)
            rstd_tile(tt, sq, rstd[:], "n_rstd")
            for k in range(KC):
                stt(hT[:, k, sl], xT[:, k, sl], gains[:, gcol + k:gcol + k + 1], rstd[:], ALU.mult, ALU.mult,
                    xk([k], [tt]) + ["n_rstd", "gains"], hk([k], [tt]))

    def final_store_tile(s, tt):
        sl = slice(tt * TW, (tt + 1) * TW)
        rstd_tile(tt, n_sq, n_rstd[:], "n_rstd")
        for k in range(KC):
            stt(ytmp[:, k, :], xT[:, k, sl], gains[:, GFIN + k:GFIN + k + 1], n_rstd[:], ALU.mult, ALU.mult,
                xk([k], [tt]) + ["n_rstd", "gains"], [("ytmp", k)])
        for jj in range(4):
            j = tt * 4 + jj
            sl_ = j % 2
            for half in range(2):
                bi, b = nbank()
                bv = v3(b[:], 128)
                fw.pe_group([tr(bv[:, kk, :], ytmp[:, half * 4 + kk, jj * 128:(jj + 1) * 128]) for kk in range(4)],
                            [("ytmp", half * 4 + kk) for kk in range(4)] + ["ident"], [("ps", bi)])
                copy("act" if half == 0 else "dve", ystg[sl_][:, half * 512:(half + 1) * 512], b[:],
                     [("ps", bi)], [("ystg", sl_, half)])
            fw.dma("pool", y_d[s, j * 128:(j + 1) * 128, :], ystg[sl_][:], [("ystg", sl_, 0), ("ystg", sl_, 1)], ())

    def ffn(i):
        main_norm(GFFN + i * 8, f_sq, f_rstd)
        for gi, (c0, ncn) in enumerate(FFN_GROUPS):
            sl_ = gi % 2
            f0, f1 = c0 * 128, (c0 + ncn) * 128
            fw.dma("pool", f_wg[sl_][:, :, 0:ncn * 128], wfg_d[i].rearrange("(k p) f -> p k f", p=128)[:, :, f0:f1],
                   (), [("wg", sl_)])
            fw.dma("pool", f_wu[sl_][:, :, 0:ncn * 128], wfu_d[i].rearrange("(k p) f -> p k f", p=128)[:, :, f0:f1],
                   (), [("wu", sl_)])
            fw.dma("pool", f_wd[sl_][:, 0:ncn, :], wfd_d[i][f0:f1, :].rearrange("(c p) d -> p c d", p=128),
                   (), [("wd", sl_)])
            order = [(c, tt) for tt in range(NTT) for c in range(ncn)] if gi == 0 else \
                    [(c, tt) for c in range(ncn) for tt in range(NTT)]
            for c, tt in order:
                if True:
                    sl = slice(tt * TW, (tt + 1) * TW)
                    big, bg = nbank()
                    fw.pe_group([mm(bg[:], f_wg[sl_][:, k, c * 128:(c + 1) * 128], hT[:, k, sl], k == 0, k == KC - 1)
                                 for k in range(KC)], hk(ALLK, [tt]) + [("wg", sl_)], [("ps", big)])
                    biu, bu = nbank()
                    fw.pe_group([mm(bu[:], f_wu[sl_][:, k, c * 128:(c + 1) * 128], hT[:, k, sl], k == 0, k == KC - 1)
                                 for k in range(KC)], hk(ALLK, [tt]) + [("wu", sl_)], [("ps", biu)])
                    sgi = (c * NTT + tt) % 2
                    act(f_sg[sgi][:], bg[:], AF.Silu, [("ps", big)], [("sg", sgi)])
                    tt_(f_act[:, c, sl], f_sg[sgi][:], bu[:], ALU.mult, [("sg", sgi), ("ps", biu)], [("act", c, tt)])
            for tt in range(NTT):
                sl = slice(tt * TW, (tt + 1) * TW)
                for dc in range(KC):
                    bi, b = nbank()
                    fw.pe_group([mm(b[:], f_wd[sl_][:, c, dc * 128:(dc + 1) * 128], f_act[:, c, sl], c == 0, c == ncn - 1)
                                 for c in range(ncn)], [("act", c, tt) for c in range(ncn)] + [("wd", sl_)], [("ps", bi)])
                    tt_(xT[:, dc, sl], xT[:, dc, sl], b[:], ALU.add, xk([dc], [tt]) + [("ps", bi)], xk([dc], [tt]))

    def pool_mixer(i):
        j = i // 2
        fw.dma("pool", p_w[:], wpool_d[j].rearrange("g (c p) d -> p g c d", p=128), (), ["p_w"])
        memset("pool", p_hA[:, 0:8], 0.0, ["hA"])
        memset("pool", p_hA[:, 8 + S:16 + S], 0.0, ["hA"])
        for tt in range(NTT):
            rstd_tile(tt, p_sq, p_rstd[:, tt * TW:(tt + 1) * TW], "p_rstd")
        for g, win in enumerate((2, 4, 8, 16)):
            left = win // 2
            right = win - 1 - left
            for cc in range(2):
                k = 2 * g + cc
                stt(p_hA[:, 8:8 + S], xT[:, k, :], gains[:, GMIX + i * 8 + k:GMIX + i * 8 + k + 1], p_rstd[:],
                    ALU.mult, ALU.mult, xk([k], ALLT) + ["p_rstd", "gains"], ["hA"])
                tt_(p_hB1[:, 1:HW_], p_hA[:, 1:HW_], p_hA[:, 0:HW_ - 1], ALU.add, ["hA"], ["hB1"])
                src = p_hB1
                skey = "hB1"
                if win >= 4:
                    tt_(p_hB2[:, 2:HW_ - 1], p_hB1[:, 3:HW_], p_hB1[:, 1:HW_ - 2], ALU.add, ["hB1"], ["hB2"])
                    src, skey = p_hB2, "hB2"
                if win >= 8:
                    tt_(p_hB1[:, 4:HW_ - 3], p_hB2[:, 6:HW_ - 1], p_hB2[:, 2:HW_ - 5], ALU.add, ["hB2"], ["hB1"])
                    src, skey = p_hB1, "hB1"
                if win >= 16:
                    tt_(p_hB2[:, 8:8 + S], p_hB1[:, 12:12 + S], p_hB1[:, 4:4 + S], ALU.add, ["hB1"], ["hB2"])
                    src, skey = p_hB2, "hB2"
                stt(p_mix[:, cc, :], src[:, 8:8 + S], 1.0 / win, p_hA[:, 8:8 + S], ALU.mult, ALU.subtract,
                    [skey, "hA"], [("mix", cc)])
                tt_(p_fix[:, 0:left], src[:, 8:8 + left], pinv[:, g, 0:left], ALU.mult, [skey, "pinv"], ["p_fix"])
                tt_(p_mix[:, cc, 0:left], p_fix[:, 0:left], p_hA[:, 8:8 + left], ALU.subtract,
                    ["p_fix", "hA"], [("mix", cc)])
                if right > 0:
                    tt_(p_fix[:, 8:8 + right], src[:, 8 + S - right:8 + S], pinv[:, g, 8:8 + right], ALU.mult,
                        [skey, "pinv"], ["p_fix"])
                    tt_(p_mix[:, cc, S - right:S], p_fix[:, 8:8 + right], p_hA[:, 8 + S - right:8 + S], ALU.subtract,
                        ["p_fix", "hA"], [("mix", cc)])
            for tt in range(NTT):
                sl = slice(tt * TW, (tt + 1) * TW)
                for dd in range(2):
                    k = 2 * g + dd
                    bi, b = nbank()
                    fw.pe_group([mm(b[:], p_w[:, g, cc, dd * 128:(dd + 1) * 128], p_mix[:, cc, sl], cc == 0, cc == 1)
                                 for cc in range(2)], [("mix", 0), ("mix", 1), "p_w"], [("ps", bi)])
                    stt(xT[:, k, sl], b[:], gains[:, PSC + j * 8 + k:PSC + j * 8 + k + 1], xT[:, k, sl],
                        ALU.mult, ALU.add, xk([k], [tt]) + [("ps", bi), "gains"], xk([k], [tt]))

    def gla_mixer(i):
        j = i // 2
        main_norm(GMIX + i * 8, g_sq, g_rstd)
        winv = win_d[j].rearrange("(k p) f -> p k f", p=128)
        fw.dma("pool", g_wgate[:], winv[:, :, 3072:3104], (), ["wgate"])
        for d_ in range(2):
            memset("pool", g_wz[d_][0:32, :], 0.0, [("wz", d_)])
            fw.dma("pool", g_wz[d_][16 * d_:16 * d_ + 16, :], wgu_d[j, d_], (), [("wz", d_)])
            fw.dma("pool", g_wz[d_][32:33, :], bg_d[j, d_:d_ + 1, :], (), [("wz", d_)])
        memset("pool", g_gT[32:33, :], 1.0, ["gT"])
        for tt in range(NTT):
            sl = slice(tt * TW, (tt + 1) * TW)
            bi, b = nbank()
            fw.pe_group([mm(b[0:32, :], g_wgate[:, k, :], hT[:, k, sl], k == 0, k == KC - 1) for k in range(KC)],
                        hk(ALLK, [tt]) + ["wgate"], [("ps", bi)])
            copy("act", g_gT[0:32, sl], b[0:32, :], [("ps", bi)], ["gT"])

        tris_f = (triFf, triFb)
        tris_t = (triTf, triTb)
        msks = (mskF, mskB)
        for h in range(4):
            fw.dma("pool", g_wq[:], winv[:, :, h * 128:(h + 1) * 128], (), ["wq"])
            fw.dma("pool", g_wkv[:, :, 0:128], winv[:, :, 512 + h * 128:512 + (h + 1) * 128], (), ["wkv"])
            fw.dma("pool", g_wkv[:, :, 128:384], winv[:, :, 1024 + h * 256:1024 + (h + 1) * 256], (), ["wkv"])
            fw.dma("pool", g_wr[:], winv[:, :, 2048 + h * 256:2048 + (h + 1) * 256], (), ["wr"])
            fw.dma("pool", g_wo[:], wout_d[j][h * 256:(h + 1) * 256, :].rearrange("(c p) d -> p c d", p=128), (), ["wo"])
            for tt in range(NTT):
                sl = slice(tt * TW, (tt + 1) * TW)
                zb = []
                for d_ in range(2):
                    bi, b = nbank()
                    bv = v3(b[:], 128)
                    for jj in range(4):
                        tsl = slice(tt * TW + jj * 128, tt * TW + (jj + 1) * 128)
                        fw.pe_group([mm(bv[:, jj, :], g_gT[0:33, tsl], g_wz[d_][0:33, h * 128:(h + 1) * 128], True, True)],
                                    ["gT", ("wz", d_)], [("ps", bi)])
                    zb.append((bi, b))
                for d_ in range(2):
                    bi, b = zb[d_]
                    act(g_e[d_][:], b[:], AF.Exp, [("ps", bi)], [("e", d_)], scale=-1.0)
                    act(g_sp[d_][:].rearrange("p a b -> p (a b)"), g_e[d_][:], AF.Ln, [("e", d_)], [("sp", d_)], bias=1.0)
                biq, bq = nbank()
                fw.pe_group([mm(bq[:], g_wq[:, k, :], hT[:, k, sl], k == 0, k == KC - 1) for k in range(KC)],
                            hk(ALLK, [tt]) + ["wq"], [("ps", biq)])
                bik, bk = nbank()
                fw.pe_group([mm(bk[:], g_wkv[:, k, 0:128], hT[:, k, sl], k == 0, k == KC - 1) for k in range(KC)],
                            hk(ALLK, [tt]) + ["wkv"], [("ps", bik)])
                cb = []
                for d_ in range(2):
                    bi, b = nbank()
                    bv = v3(b[:], 128)
                    for jj in range(4):
                        fw.pe_group([mm(bv[:, jj, :], g_sp[d_][:, jj, :], tris_f[d_][:], True, True)],
                                    [("sp", d_), ("tri", d_)], [("ps", bi)])
                    bi2, b2 = nbank()
                    fw.pe_group([mm(b2[:], tris_t[d_][:], g_sp[d_][:].rearrange("p a b -> p (a b)"), True, True)],
                                [("sp", d_), ("tri", 2 + d_)], [("ps", bi2)])
                    cb.append((bi, b, bi2, b2))
                for d_ in range(2):
                    bi, b, bi2, b2 = cb[d_]
                    act(g_Eq[d_][:], b[:], AF.Exp, [("ps", bi)], [("Eq", d_)])
                    act(g_Ek[d_][:], b[:], AF.Exp, [("ps", bi)], [("Ek", d_)], scale=-1.0)
                    act(g_Ekd[d_][:].rearrange("p a b -> p (a b)"), b2[:], AF.Exp, [("ps", bi2)], [("Ekd", d_)])
                for d_ in range(2):
                    stt(g_qd[d_][:, sl], bq[:], 128.0 ** -0.5, g_Eq[d_][:], ALU.mult, ALU.mult,
                        [("ps", biq), ("Eq", d_)], [("qd", d_)])
                for d_ in range(2):
                    tt_(g_ki[d_][:, sl], bk[:], g_Ek[d_][:], ALU.mult, [("ps", bik), ("Ek", d_)], [("ki", d_)])
                for d_ in range(2):
                    col = 127 if d_ == 0 else 0
                    copy("dve", g_dec[d_][:, tt * 4:(tt + 1) * 4], v3(g_Eq[d_][:], 128)[:, :, col], [("Eq", d_)], [("dec", d_)])
                for jj in range(4):
                    jt = tt * 4 + jj
                    tsl = slice(jt * 128, (jt + 1) * 128)
                    bi, b = nbank()
                    fw.pe_group([mm(b[:, 0:128], hT[:, k, tsl], g_wkv[:, k, 0:128], k == 0, k == KC - 1) for k in range(KC)],
                                hk(ALLK, [tt]) + ["wkv"], [("ps", bi)])
                    for d_ in range(2):
                        tt_(g_kd[d_][:, jt, :], b[:, 0:128], g_Ekd[d_][:, jj, :], ALU.mult,
                            [("ps", bi), ("Ekd", d_)], [("kd", d_)])
                    bi, b = nbank()
                    fw.pe_group([mm(b[:, 0:256], hT[:, k, tsl], g_wkv[:, k, 128:384], k == 0, k == KC - 1) for k in range(KC)],
                                hk(ALLK, [tt]) + ["wkv"], [("ps", bi)])
                    copy("act", g_v[:, jt, :], b[:, 0:256], [("ps", bi)], ["v"])
            for d_ in range(2):
                memset("dve", g_S32[d_][0][:], 0.0, [("S32", d_, 0)])
                memset("dve", g_Sbf[d_][0][:], 0.0, [("Sbf", d_, 0)])
            st_i = [0, 0]
            sb_i = [0, 0]
            for step in range(16):
                pend = []
                for d_ in range(2):
                    jt = step if d_ == 0 else 15 - step
                    tsl = slice(jt * 128, (jt + 1) * 128)
                    bi, b = nbank()
                    fw.pe_group([mm(b[:, 0:128], g_ki[d_][:, tsl], g_qd[d_][:, tsl], True, True)],
                                [("ki", d_), ("qd", d_)], [("ps", bi)])
                    biu, bu = nbank()
                    fw.pe_group([mm(bu[:, 0:256], g_kd[d_][:, jt, :], g_v[:, jt, :], True, True)],
                                [("kd", d_), "v"], [("ps", biu)])
                    smi = step % 2
                    sm = g_sm[d_][smi]
                    tt_(sm[:], b[:, 0:128], msks[d_][:], ALU.mult, [("ps", bi), ("tri", 4 + d_)], [("sm", d_, smi)])
                    pend.append((d_, jt, tsl, biu, bu, smi, sm))
                for d_, jt, tsl, biu, bu, smi, sm in pend:
                    bio, bo = nbank()
                    bov = v3(bo[:, 0:256], 128)
                    sb = g_Sbf[d_][sb_i[d_]]
                    fns = []
                    for vc in range(2):
                        fns.append(mm(bov[:, vc, :], g_v[:, jt, vc * 128:(vc + 1) * 128], sm[:], True, False))
                        fns.append(mm(bov[:, vc, :], sb[:, vc * 128:(vc + 1) * 128], g_qd[d_][:, tsl], False, True))
                    fw.pe_group(fns, ["v", ("sm", d_, smi), ("Sbf", d_, sb_i[d_]), ("qd", d_)], [("ps", bio)])
                    cur = st_i[d_]
                    nxt = 1 - cur
                    stt(g_S32[d_][nxt][:], g_S32[d_][cur][:], g_dec[d_][:, jt:jt + 1], bu[:, 0:256],
                        ALU.mult, ALU.add, [("S32", d_, cur), ("dec", d_), ("ps", biu)], [("S32", d_, nxt)])
                    st_i[d_] = nxt
                    nsb = (sb_i[d_] + 1) % 3
                    copy("act", g_Sbf[d_][nsb][:], g_S32[d_][nxt][:], [("S32", d_, nxt)], [("Sbf", d_, nsb)])
                    sb_i[d_] = nsb
                    if step < 8:
                        copy("act", g_oacc[:, :, tsl], bov, [("ps", bio)], [("oacc", jt)])
                    else:
                        tt_(g_oacc[:, :, tsl], g_oacc[:, :, tsl], bov, ALU.add, [("ps", bio), ("oacc", jt)], [("oacc", jt)])
            def p4a(tt):
                sl = slice(tt * TW, (tt + 1) * TW)
                okeys = [("oacc", jt) for jt in range(tt * 4, tt * 4 + 4)]
                act(g_sq2[:], g_oacc[:, :, sl], AF.Square, okeys, ["sq2"])
                bi, b = nbank()
                fw.pe_group([mm(b[:], ones_v[:], g_sq2[:, vc, :], vc == 0, vc == 1) for vc in range(2)],
                            ["sq2", "ones_v"], [("ps", bi)])
                act(g_rstd2[:], b[:], AF.Ln, [("ps", bi)], ["rstd2"], bias=EPS)
                act(g_rstd2[:], g_rstd2[:], AF.Exp, ["rstd2"], ["rstd2"], scale=-0.5)
                for vc in range(2):
                    bi, b = nbank()
                    fw.pe_group([mm(b[:], g_wr[:, k, vc * 128:(vc + 1) * 128], hT[:, k, sl], k == 0, k == KC - 1)
                                 for k in range(KC)], hk(ALLK, [tt]) + ["wr"], [("ps", bi)])
                    act(g_sr[vc][:], b[:], AF.Silu, [("ps", bi)], [("sr", vc)])
                    stt(g_t1[:], g_oacc[:, vc, sl], gains[:, HG + j * 2 + vc:HG + j * 2 + vc + 1], g_rstd2[:],
                        ALU.mult, ALU.mult, okeys + ["rstd2", "gains"], ["t1"])
                    tt_(g_go[:, vc, sl], g_t1[:], g_sr[vc][:], ALU.mult, ["t1", ("sr", vc)], [("go", vc, tt)])

            def p4b(tt):
                sl = slice(tt * TW, (tt + 1) * TW)
                for dc in range(KC):
                    bi, b = nbank()
                    fw.pe_group([mm(b[:], g_wo[:, vc, dc * 128:(dc + 1) * 128], g_go[:, vc, sl], vc == 0, vc == 1)
                                 for vc in range(2)], [("go", 0, tt), ("go", 1, tt), "wo"], [("ps", bi)])
                    tt_(xT[:, dc, sl], xT[:, dc, sl], b[:], ALU.add, xk([dc], [tt]) + [("ps", bi)], xk([dc], [tt]))

            p4a(0)
            for tt in range(1, NTT):
                p4a(tt)
                p4b(tt - 1)
            p4b(NTT - 1)

    setup_consts()
    for j in range(16):
        load_x_tile(0, j)
    for s in range(n_seq):
        if s > 0:
            fw.new_epoch()
        for i in layers:
            if i % 2 == 0:
                pool_mixer(i)
            else:
                gla_mixer(i)
            ffn(i)
        for tt in range(NTT):
            final_store_tile(s, tt)
            if s + 1 < n_seq:
                for j in range(4 * tt, 4 * tt + 4):
                    load_x_tile(s + 1, j)
    fw.barrier()

    with nc.Block() as block:
        @block.tensor
        def _(e):
            fw.replay("pe", e)

        @block.scalar
        def _(e):
            fw.replay("act", e)

        @block.vector
        def _(e):
            fw.replay("dve", e)

        @block.gpsimd
        def _(e):
            fw.replay("pool", e)

        @block.sync
        def _(e):
            fw.replay("sp", e)
    stack.close()
    return nc


_WNAMES = ["norm_mix", "norm_ffn", "norm_final", "w_pool", "pool_scale", "w_gla_in", "w_gate_up", "b_gate",
           "gla_head_norm", "w_gla_out", "w_ffn_gate", "w_ffn_up", "w_ffn_down"]


def kernel(**inputs):
    x = np.ascontiguousarray(np.asarray(inputs["x"], dtype=np.float32))
    B = x.shape[0]
    per = B // N_CORES
    nc = build(n_seq=per)
    w = {k: np.ascontiguousarray(np.asarray(inputs[k], dtype=np.float32)) for k in _WNAMES}
    in_maps = []
    for c in range(N_CORES):
        m = {"x": x[c * per:(c + 1) * per]}
        m.update(w)
        in_maps.append(m)
    res = run_bass_kernel_spmd(nc, in_maps, core_ids=list(range(N_CORES)))
    return np.concatenate([np.asarray(r["y"], dtype=np.float32) for r in res.results], axis=0)
```
